# Optimizing a Trainium2 kernel written in Bass

```python
import jax, jax.numpy as jnp
from jax import lax
import numpy as np

D_MODEL = 1024
BATCH = 2
SEQ = 16384
DEPTH = 2

D_PLE = 256
HEAD_DIM = 64
N_HEADS_SGU = 4
N_HEADS_RET = 6
N_HEADS_FOX = 6
W_SGU = N_HEADS_SGU * HEAD_DIM
W_RET = N_HEADS_RET * HEAD_DIM
W_FOX = N_HEADS_FOX * HEAD_DIM
D_MIX = W_SGU + W_RET + W_FOX
CHUNK = 128
D_FF = 4 * D_MODEL
CONV_WIDTH = 3
ROPE_BASE = 10000.0
RMS_EPS = 1e-6
IN_SPLIT_SIZES = (W_SGU, W_SGU, W_RET, W_RET, W_RET, W_RET, W_FOX, W_FOX, W_FOX, N_HEADS_FOX)
N_IN = sum(IN_SPLIT_SIZES)
IN_SPLIT_POINTS = tuple(int(v) for v in np.cumsum(IN_SPLIT_SIZES)[:-1])

kernel_name = "hymba_style_sgu_retention_fox_block"


def rmsnorm(x, g):
    xf = x.astype(jnp.float32)
    y = xf * lax.rsqrt(jnp.mean(xf * xf, axis=-1, keepdims=True) + RMS_EPS)
    return (y * g.astype(jnp.float32)).astype(x.dtype)


def rotary(t, positions):
    half = t.shape[-1] // 2
    inv_freq = ROPE_BASE ** (-jnp.arange(half, dtype=jnp.float32) / half)
    ang = positions.astype(jnp.float32)[..., None] * inv_freq
    cos = jnp.cos(ang)[:, :, None, :]
    sin = jnp.sin(ang)[:, :, None, :]
    t1, t2 = t[..., :half], t[..., half:]
    return jnp.concatenate([t1 * cos - t2 * sin, t1 * sin + t2 * cos], axis=-1)


def spatial_gating(u, v, v_gain, w_s, b_s):
    B, S, H, D = v.shape
    nc = S // CHUNK
    v = rmsnorm(v, v_gain).reshape(B, nc, CHUNK, H, D)
    causal = jnp.tril(jnp.ones((CHUNK, CHUNK), dtype=bool))
    w = jnp.where(causal[None], w_s, jnp.zeros_like(w_s))
    s = jnp.einsum('hts,bnshd->bnthd', w, v) + b_s.T[None, None, :, :, None]
    return u * s.reshape(B, S, H, D)


def retention_chunkwise(q, k, v):
    B, S, H, D = q.shape
    nc = S // CHUNK
    log_g = jnp.log(1.0 - 2.0 ** (-5.0 - jnp.arange(H, dtype=jnp.float32)))
    idx = jnp.arange(CHUNK, dtype=jnp.float32)
    diff = idx[:, None] - idx[None, :]
    decay = jnp.where(diff >= 0, jnp.exp(log_g[:, None, None] * jnp.maximum(diff, 0.0)), 0.0)
    q_decay = jnp.exp(log_g[:, None] * (idx + 1.0))[None, :, :, None]
    k_decay = jnp.exp(log_g[:, None] * (CHUNK - 1.0 - idx))[None, :, :, None]
    chunk_decay = jnp.exp(log_g * CHUNK)[None, :, None, None]

    def to_chunks(t):
        return t.reshape(B, nc, CHUNK, H, D).transpose(1, 0, 3, 2, 4)

    def step(R, inp):
        q_i, k_i, v_i = inp
        inner = jnp.einsum('bhtd,bhsd->bhts', q_i, k_i) * decay
        o = jnp.einsum('bhts,bhse->bhte', inner, v_i)
        o = o + jnp.einsum('bhtd,bhde->bhte', q_i, R) * q_decay
        R = R * chunk_decay + jnp.einsum('bhsd,bhse->bhde', k_i * k_decay, v_i)
        return R, o

    R0 = jnp.zeros((B, H, D, D), dtype=jnp.float32)
    _, o = lax.scan(step, R0, (to_chunks(q), to_chunks(k), to_chunks(v)))
    return o.transpose(1, 0, 3, 2, 4).reshape(B, S, H, D)


def forgetting_attention(q, k, v, log_f):
    B, S, H, D = q.shape
    nb = S // CHUNK
    scale = D ** -0.5
    c = jnp.cumsum(log_f, axis=1)
    cT = c.transpose(0, 2, 1)
    qb = q.reshape(B, nb, CHUNK, H, D).transpose(1, 0, 2, 3, 4)
    cb = cT.reshape(B, H, nb, CHUNK).transpose(2, 0, 1, 3)
    starts = jnp.arange(nb, dtype=jnp.int32) * CHUNK
    key_pos = jnp.arange(S, dtype=jnp.int32)

    def one_block(args):
        q_i, c_i, start = args
        s = jnp.einsum('bqhd,bkhd->bhqk', q_i, k).astype(jnp.float32) * scale
        s = s + c_i[..., None] - cT[:, :, None, :]
        q_pos = start + jnp.arange(CHUNK, dtype=jnp.int32)
        mask = key_pos[None, :] <= q_pos[:, None]
        s = jnp.where(mask[None, None], s, -jnp.inf)
        prob = jax.nn.softmax(s, axis=-1)
        return jnp.einsum('bhqk,bkhd->bqhd', prob.astype(v.dtype), v)

    out = lax.map(one_block, (qb, cb, starts))
    return out.transpose(1, 0, 2, 3, 4).reshape(B, S, H, D)


def conv_gated_mlp(h, w_gate, w_up, conv_w, conv_b, w_down):
    S = h.shape[1]
    g = h @ w_gate
    gp = jnp.pad(g, ((0, 0), (CONV_WIDTH - 1, 0), (0, 0)))
    conv = conv_b + gp[:, 0:S] * conv_w[0]
    for j in range(1, CONV_WIDTH):
        conv = conv + gp[:, j:j + S] * conv_w[j]
    act = jax.nn.gelu(conv, approximate=True) * (h @ w_up)
    return act @ w_down


def setup_inputs(seed: int = 0) -> dict:
    key = jax.random.key(seed)
    ks = jax.random.split(key, 24)
    f32 = jnp.float32

    def nrm(k, shape, scale):
        return jax.random.normal(k, shape, f32) * scale

    def gain(k, shape):
        return 1.0 + 0.05 * jax.random.normal(k, shape, f32)

    return {
        "x": nrm(ks[0], (BATCH, SEQ, D_MODEL), 1.0),
        "p": nrm(ks[1], (DEPTH, BATCH, SEQ, D_PLE), 1.0),
        "positions": jnp.broadcast_to(jnp.arange(SEQ, dtype=jnp.int32)[None, :], (BATCH, SEQ)),
        "mix_pre_g": gain(ks[2], (DEPTH, D_MODEL)),
        "w_in": nrm(ks[3], (DEPTH, D_MODEL, N_IN), D_MODEL ** -0.5),
        "sgu_v_g": gain(ks[4], (DEPTH, N_HEADS_SGU, HEAD_DIM)),
        "sgu_w": nrm(ks[5], (DEPTH, N_HEADS_SGU, CHUNK, CHUNK), CHUNK ** -0.5),
        "sgu_b": 1.0 + 0.01 * jax.random.normal(ks[6], (DEPTH, N_HEADS_SGU, CHUNK), f32),
        "fox_b_f": 3.0 + 0.5 * jax.random.normal(ks[7], (DEPTH, N_HEADS_FOX), f32),
        "w_o": nrm(ks[8], (DEPTH, D_MIX, D_MODEL), D_MIX ** -0.5),
        "mix_post_g": gain(ks[9], (DEPTH, D_MODEL)),
        "ffn_pre_g": gain(ks[10], (DEPTH, D_MODEL)),
        "w_gate": nrm(ks[11], (DEPTH, D_MODEL, D_FF), D_MODEL ** -0.5),
        "w_up": nrm(ks[12], (DEPTH, D_MODEL, D_FF), D_MODEL ** -0.5),
        "conv_w": nrm(ks[13], (DEPTH, CONV_WIDTH, D_FF), CONV_WIDTH ** -0.5),
        "conv_b": nrm(ks[14], (DEPTH, D_FF), 0.01),
        "w_down": nrm(ks[15], (DEPTH, D_FF, D_MODEL), D_FF ** -0.5),
        "ffn_post_g": gain(ks[16], (DEPTH, D_MODEL)),
        "ple_pre_g": gain(ks[17], (DEPTH, D_MODEL)),
        "w_ple_gate": nrm(ks[18], (DEPTH, D_MODEL, D_MODEL), D_MODEL ** -0.5),
        "w_ple_proj": nrm(ks[19], (DEPTH, D_PLE, D_MODEL), D_PLE ** -0.5),
        "ple_post_g": gain(ks[20], (DEPTH, D_MODEL)),
    }


def reference(x, p, positions, mix_pre_g, w_in, sgu_v_g, sgu_w, sgu_b, fox_b_f, w_o,
              mix_post_g, ffn_pre_g, w_gate, w_up, conv_w, conv_b, w_down, ffn_post_g,
              ple_pre_g, w_ple_gate, w_ple_proj, ple_post_g):
    B, S, _ = x.shape
    dt = x.dtype
    for i in range(DEPTH):
        h = rmsnorm(x, mix_pre_g[i])
        z = h @ w_in[i]
        (a_u, a_v, r_q, r_k, r_v, r_g,
         f_q, f_k, f_v, f_f) = jnp.split(z, IN_SPLIT_POINTS, axis=-1)

        a_u = jax.nn.gelu(a_u, approximate=True).reshape(B, S, N_HEADS_SGU, HEAD_DIM)
        a_v = jax.nn.gelu(a_v, approximate=True).reshape(B, S, N_HEADS_SGU, HEAD_DIM)
        out_a = spatial_gating(a_u, a_v, sgu_v_g[i], sgu_w[i], sgu_b[i]).reshape(B, S, W_SGU)

        rq = rotary(r_q.astype(jnp.float32).reshape(B, S, N_HEADS_RET, HEAD_DIM), positions)
        rk = rotary(r_k.astype(jnp.float32).reshape(B, S, N_HEADS_RET, HEAD_DIM), positions) * (HEAD_DIM ** -0.5)
        rv = r_v.astype(jnp.float32).reshape(B, S, N_HEADS_RET, HEAD_DIM)
        ro = retention_chunkwise(rq, rk, rv)
        ro = ro * lax.rsqrt(jnp.mean(ro * ro, axis=-1, keepdims=True) + RMS_EPS)
        out_b = (jax.nn.silu(r_g.astype(jnp.float32)) * ro.reshape(B, S, W_RET)).astype(dt)

        log_f = jax.nn.log_sigmoid(f_f.astype(jnp.float32) + fox_b_f[i].astype(jnp.float32))
        out_c = forgetting_attention(
            f_q.reshape(B, S, N_HEADS_FOX, HEAD_DIM),
            f_k.reshape(B, S, N_HEADS_FOX, HEAD_DIM),
            f_v.reshape(B, S, N_HEADS_FOX, HEAD_DIM),
            log_f).reshape(B, S, W_FOX).astype(dt)

        mix = jnp.concatenate([out_a.astype(dt), out_b, out_c], axis=-1) @ w_o[i]
        x = x + rmsnorm(mix, mix_post_g[i])

        h = rmsnorm(x, ffn_pre_g[i])
        f = conv_gated_mlp(h, w_gate[i], w_up[i], conv_w[i], conv_b[i], w_down[i])
        x = x + rmsnorm(f, ffn_post_g[i])

        gate = jax.nn.sigmoid(rmsnorm(x, ple_pre_g[i]) @ w_ple_gate[i])
        e = p[i] @ w_ple_proj[i]
        x = x + rmsnorm(e * gate, ple_post_g[i])
    return x
```

```python
import math
from contextlib import ExitStack
import numpy as np
import ml_dtypes
import concourse.bass as bass
import concourse.mybir as mybir
from concourse.bass_utils import run_bass_kernel_spmd

F32 = mybir.dt.float32
BF16 = mybir.dt.bfloat16
I32 = mybir.dt.int32
AF = mybir.ActivationFunctionType
ALU = mybir.AluOpType
AX = mybir.AxisListType

D = 1024
KC = D // 128
DFF = 4096
FC = DFF // 128
DPLE = 256
EPS = 1e-6
NCORES = 8
TT = 512


class Buf:
    __slots__ = ("name", "last_write", "readers")

    def __init__(self, name=""):
        self.name = name
        self.last_write = None
        self.readers = []


class Op:
    __slots__ = ("eng", "fn", "deps", "is_dma", "need_sig", "sem", "val", "slot")

    def __init__(self, eng, fn, is_dma):
        self.eng = eng
        self.fn = fn
        self.deps = []
        self.is_dma = is_dma
        self.need_sig = is_dma
        self.sem = None
        self.val = None
        self.slot = None


class Sched:
    ENGS = ("pe", "act", "dve", "pool", "sp")
    NSLOT = 8

    def __init__(self, nc, es):
        self.nc = nc
        self.es = es
        self.ops = {e: [] for e in self.ENGS}
        self.h = {"pe": nc.tensor, "act": nc.scalar, "dve": nc.vector, "pool": nc.gpsimd, "sp": nc.sync}
        self.nbuf = 0

    def buf(self, name=""):
        return Buf(name)

    def op(self, eng, fn, reads=(), writes=(), dma=False):
        o = Op(eng, fn, dma)
        deps = set()
        for b in reads:
            if b.last_write is not None:
                deps.add(b.last_write)
        for b in writes:
            if b.last_write is not None:
                deps.add(b.last_write)
            for r in b.readers:
                deps.add(r)
        deps.discard(o)
        for d in deps:
            if d.eng == "pe" and eng == "pe" and not d.is_dma and not dma:
                continue
            o.deps.append(d)
            d.need_sig = True
        for b in reads:
            b.readers.append(o)
        for b in writes:
            b.last_write = o
            b.readers = []
        self.ops[eng].append(o)
        return o

    def emit(self):
        nc, es = self.nc, self.es
        sems = {e: es.enter_context(nc.semaphore("s_" + e)) for e in self.ENGS}
        dsems = {e: [es.enter_context(nc.semaphore("d_%s%d" % (e, i))) for i in range(self.NSLOT)]
                 for e in ("sp", "pool", "act")}
        slot_prev = {}
        for e in self.ENGS:
            cnt = 0
            dcnt = [0] * self.NSLOT
            nd = 0
            for o in self.ops[e]:
                if o.is_dma:
                    s = nd % self.NSLOT
                    nd += 1
                    dcnt[s] += 1
                    o.sem = dsems[e][s]
                    o.val = 16 * dcnt[s]
                    o.slot = (o.sem, 16 * (dcnt[s] - 1))
                elif o.need_sig:
                    cnt += 1
                    o.sem = sems[e]
                    o.val = cnt
        block = es.enter_context(nc.Block())

        def run(e):
            def body(eng):
                waited = {}
                for o in self.ops[e]:
                    ws = [(d.sem, d.val) for d in o.deps]
                    if o.is_dma and o.slot[1] > 0:
                        ws.append(o.slot)
                    best = {}
                    for s, v in ws:
                        k = id(s)
                        if v > waited.get(k, 0) and v > best.get(k, (None, 0))[1]:
                            best[k] = (s, v)
                    for k, (s, v) in best.items():
                        eng.wait_ge(s, v)
                        waited[k] = v
                    ins = o.fn(eng)
                    if o.need_sig:
                        ins.then_inc(o.sem, 16 if o.is_dma else 1)
            return body

        block.tensor(run("pe"))
        block.scalar(run("act"))
        block.vector(run("dve"))
        block.gpsimd(run("pool"))
        block.sync(run("sp"))


class Ctx:
    def __init__(self, name):
        self.nc = bass.Bass("TRN2", target_bir_lowering=False)
        self.es = ExitStack()
        self.s = Sched(self.nc, self.es)
        self.n = 0
        self.final_dmas = []

    def sb(self, shape, dt, name=None):
        self.n += 1
        t = self.es.enter_context(self.nc.sbuf_tensor(name or ("t%d" % self.n), list(shape), dt))
        return t

    def ps(self, shape=(128, 512), dt=F32, name=None):
        self.n += 1
        return self.es.enter_context(self.nc.psum_tensor(name or ("p%d" % self.n), list(shape), dt))

    def din(self, name, shape, dt):
        return self.nc.dram_tensor(name, list(shape), dt, kind="ExternalInput").ap()

    def dout(self, name, shape, dt):
        return self.nc.dram_tensor(name, list(shape), dt, kind="ExternalOutput").ap()

    def dscratch(self, name, shape, dt):
        return self.nc.dram_tensor(name, list(shape), dt, kind="Internal").ap()

    def dma(self, q, out, in_, reads=(), writes=(), final=False):
        o = self.s.op(q, lambda e: e.dma_start(out=out, in_=in_), reads, writes, dma=True)
        if final:
            self.final_dmas.append(o)
        return o

    def mm(self, out, lhsT, rhs, start, stop, reads=(), writes=()):
        return self.s.op("pe", lambda e: e.matmul(out, lhsT, rhs, start=start, stop=stop), reads, writes)

    def act(self, out, in_, func, reads=(), writes=(), bias=None, scale=None, eng="act"):
        kw = {}
        if bias is not None:
            kw["bias"] = bias
        if scale is not None:
            kw["scale"] = scale
        return self.s.op("act", lambda e: e.activation(out=out, in_=in_, func=func, **kw), reads, writes)

    def tt(self, eng, out, in0, in1, op, reads=(), writes=()):
        return self.s.op(eng, lambda e: e.tensor_tensor(out=out, in0=in0, in1=in1, op=op), reads, writes)

    def ts(self, eng, out, in0, s1, s2, op0, op1=None, reads=(), writes=()):
        if op1 is None:
            return self.s.op(eng, lambda e: e.tensor_scalar(out=out, in0=in0, scalar1=s1, scalar2=None, op0=op0),
                             reads, writes)
        return self.s.op(eng, lambda e: e.tensor_scalar(out=out, in0=in0, scalar1=s1, scalar2=s2, op0=op0, op1=op1),
                         reads, writes)

    def stt(self, eng, out, in0, scalar, in1, op0, op1, reads=(), writes=()):
        return self.s.op(eng, lambda e: e.scalar_tensor_tensor(out=out, in0=in0, scalar=scalar, in1=in1,
                                                               op0=op0, op1=op1), reads, writes)

    def copy(self, eng, out, in_, reads=(), writes=()):
        if eng == "act":
            return self.s.op("act", lambda e: e.activation(out=out, in_=in_, func=AF.Copy), reads, writes)
        return self.s.op(eng, lambda e: e.tensor_copy(out=out, in_=in_), reads, writes)

    def recip(self, out, in_, reads=(), writes=()):
        return self.s.op("dve", lambda e: e.reciprocal(out=out, in_=in_), reads, writes)

    def memset(self, eng, ap, val, writes=()):
        return self.s.op(eng, lambda e: e.memset(ap, val), (), writes)

    def finish(self):
        bdone = [Buf("done")]
        if self.final_dmas:
            fd = list(self.final_dmas)
            o = self.s.op("sp", lambda e: e.nop(), (), ())
            for d in fd:
                o.deps.append(d)
        self.s.emit()
        self.es.close()
        return self.nc


class Common:
    def __init__(self, c: Ctx):
        self.c = c
        self.ones = c.sb([128, 128], BF16, "ones_bf")
        self.b_ones = Buf("ones")
        c.memset("pool", self.ones[:], 1.0, [self.b_ones])
        self.epsb = c.sb([128, 1], F32, "eps_b")
        self.b_eps = Buf("eps")
        c.memset("pool", self.epsb[:], EPS, [self.b_eps])
        self.sq = c.sb([128, KC, TT], BF16, "nsq")
        self.b_sq = Buf("sq")
        self.ssp = c.ps(name="ss_ps")
        self.b_ssp = Buf("ssp")
        self.rt = c.sb([128, TT], F32, "nrt")
        self.b_rt = Buf("rt")

    def rstd(self, src, b_src, out, b_out, nchunk=KC, width=TT, dim=D, rows=128):
        c = self.c
        c.act(self.sq[0:rows, 0:nchunk, 0:width], src, AF.Square, [b_src], [self.b_sq])
        for k in range(nchunk):
            c.mm(self.ssp[0:rows, 0:width], self.ones[0:rows, 0:rows], self.sq[0:rows, k, 0:width],
                 k == 0, k == nchunk - 1, [self.b_sq, self.b_ones], [self.b_ssp])
        c.act(self.rt[0:rows, 0:width], self.ssp[0:rows, 0:width], AF.Sqrt, [self.b_ssp, self.b_eps], [self.b_rt],
              bias=self.epsb[0:rows, :], scale=1.0 / dim)
        c.recip(out, self.rt[0:rows, 0:width], [self.b_rt], [b_out])


def load_vec_fm(c, q, dram_ap, nchunk, name):
    t = c.sb([128, nchunk], F32, name)
    b = Buf(name)
    c.dma(q, t[:], dram_ap, (), [b])
    return t, b


def build_C(NT):
    c = Ctx("C")
    NP = 1
    PT = NT
    ntile = NT // TT
    xT_in = c.din("xT", [D, NT], F32)
    po_fo = c.din("po_fo", [4, 384, NT], F32)
    po_fs = c.din("po_fs", [4, 6, NT], F32)
    po_ro = c.din("po_ro", [4, 384, NT], F32)
    oaT = c.din("out_aT", [256, NT], BF16)
    rgT = c.din("rgT", [384, NT], BF16)
    h_x = c.din("h_x", [D, 2], F32)
    h_fo = c.din("h_fo", [4, 384, 2], F32)
    h_fs = c.din("h_fs", [4, 6, 2], F32)
    h_ro = c.din("h_ro", [4, 384, 2], F32)
    h_oa = c.din("h_oa", [256, 2], BF16)
    h_rg = c.din("h_rg", [384, 2], BF16)
    w_o = c.din("w_o", [KC, 128, KC, 128], F32)
    g_mpost = c.din("mix_post_g", [128, KC], F32)
    selb_d = c.din("selb", [6, 3, 128], F32)
    blk_d = c.din("blk", [128, 128], F32)
    pT = c.din("pT", [DPLE, NT], F32)
    wg = c.din("w_gate", [FC, 128, KC, 128], F32)
    wu = c.din("w_up", [FC, 128, KC, 128], F32)
    wd = c.din("w_down", [KC, 128, FC, 128], F32)
    wpg = c.din("w_ple_gate", [KC, 128, KC, 128], F32)
    wpp = c.din("w_ple_proj", [KC, 128, 2, 128], F32)
    convw = c.din("conv_w", [128, FC, 3], F32)
    convb = c.din("conv_b", [128, FC], F32)
    g_pre = c.din("ffn_pre_g", [128, KC], F32)
    g_post = c.din("ffn_post_g", [128, KC], F32)
    g_ppre = c.din("ple_pre_g", [128, KC], F32)
    g_ppost = c.din("ple_post_g", [128, KC], F32)
    outT = c.dout("outT", [D, NT], F32)
    wg_s = c.dscratch("wg_s", [FC, 128, KC * 128], BF16)
    wu_s = c.dscratch("wu_s", [FC, 128, KC * 128], BF16)
    wd_s = c.dscratch("wd_s", [KC, 128, FC * 128], BF16)

    cm = Common(c)
    gpre, b_gpre = load_vec_fm(c, "sp", g_pre, KC, "gpre")
    gpost, b_gpost = load_vec_fm(c, "sp", g_post, KC, "gpost")
    gppre, b_gppre = load_vec_fm(c, "sp", g_ppre, KC, "gppre")
    gppost, b_gppost = load_vec_fm(c, "sp", g_ppost, KC, "gppost")
    cw = c.sb([128, FC, 3], F32, "cw")
    b_cw = Buf()
    c.dma("sp", cw[:], convw, (), [b_cw])
    cb = c.sb([128, FC], F32, "cb")
    b_cb = Buf()
    c.dma("sp", cb[:], convb, (), [b_cb])
    gmpost, b_gmpost = load_vec_fm(c, "sp", g_mpost, KC, "gmpost")
    wo_t = c.sb([128, KC, KC * 128], BF16, "wo")
    b_wo = Buf()
    for o in range(KC):
        c.dma("pool", wo_t[:, o, :], w_o[o].rearrange("p k c -> p (k c)"), (), [b_wo])
    selb = c.sb([6, 3, 128], F32, "selb_sb")
    b_selb = Buf()
    c.dma("sp", selb[:], selb_d, (), [b_selb])
    blk = c.sb([128, 128], BF16, "blk_sb")
    b_blk = Buf()
    c.dma("pool", blk[:], blk_d, (), [b_blk])
    wpg_t = c.sb([128, KC, KC * 128], BF16, "wpg")
    b_wpg = Buf()
    for o in range(KC):
        c.dma("pool", wpg_t[:, o, :], wpg[o].rearrange("p k c -> p (k c)"), (), [b_wpg])
    wpp_t = c.sb([128, KC, 2 * 128], BF16, "wpp")
    b_wpp = Buf()
    for o in range(KC):
        c.dma("pool", wpp_t[:, o, :], wpp[o].rearrange("p k c -> p (k c)"), (), [b_wpp])
    NCV = 2
    wdt = [c.sb([128, FC * 128], BF16, "wdt%d" % i) for i in range(2)]
    b_wdt = [Buf() for _ in range(2)]
    cv, b_cv = wdt, b_wdt
    b_wgs = [Buf() for _ in range(FC)]
    b_wus = [Buf() for _ in range(FC)]
    b_wds = [Buf() for _ in range(KC)]
    ci = 0
    for (src, dst, bl, n, w) in ((wg, wg_s, b_wgs, FC, KC * 128), (wu, wu_s, b_wus, FC, KC * 128),
                                 (wd, wd_s, b_wds, KC, FC * 128)):
        step = (FC * 128) // w
        for i0 in range(0, n, step):
            s = ci % NCV
            ci += 1
            for j in range(step):
                c.dma("pool", cv[s][:, j * w:(j + 1) * w], src[i0 + j].rearrange("p k c -> p (k c)"), (), [b_cv[s]])
            for j in range(step):
                c.dma("sp", dst[i0 + j], cv[s][:, j * w:(j + 1) * w], [b_cv[s]], [bl[i0 + j]])

    xt = [c.sb([128, KC, TT], F32, "xt0")] * 2
    b_xt = [Buf()] * 2
    hT = c.sb([128, KC, TT], BF16, "hT")
    b_hT = Buf()
    rs = c.sb([128, TT], F32, "rs")
    b_rs = Buf()
    actT = c.sb([128, FC, TT], BF16, "actT")
    b_act = [Buf() for _ in range(FC)]
    fT = c.sb([128, KC, TT], F32, "fT")
    b_fT = Buf()
    tmp = c.sb([128, TT], F32, "tmp")
    b_tmp = Buf()
    gs = [c.sb([128, TT + 2], F32, "gs%d" % i) for i in range(2)]
    b_gs = [Buf() for _ in range(2)]
    c1 = [c.sb([128, TT], F32, "c1_%d" % i) for i in range(2)]
    b_c1 = [Buf() for _ in range(2)]
    c2 = [c.sb([128, TT], F32, "c2_0")] * 2
    b_c2 = [Buf()] * 2
    gcar = c.sb([128, FC, 2], F32, "gcar")
    b_gcar = [Buf() for _ in range(FC)]
    pt = c.sb([128, 2, TT], BF16, "pt")
    b_pt = Buf()
    hx = c.sb([128, KC, 2], F32, "hx")
    b_hx = Buf()
    hh = c.sb([128, KC, 2], BF16, "hh")
    b_hh = Buf()
    hrs = c.sb([128, 2], F32, "hrs")
    b_hrs = Buf()
    NW = 2
    wgt = [c.sb([128, KC * 128], BF16, "wgt%d" % i) for i in range(NW)]
    b_wgt = [Buf() for _ in range(NW)]
    wut = [c.sb([128, KC * 128], BF16, "wut%d" % i) for i in range(NW)]
    b_wut = [Buf() for _ in range(NW)]
    gps = [c.ps(name="gps%d" % i) for i in range(2)]
    b_gps = [Buf() for _ in range(2)]
    ups = [c.ps(name="ups%d" % i) for i in range(2)]
    b_ups = [Buf() for _ in range(2)]
    ops_ = [c.ps(name="ops%d" % i) for i in range(2)]
    b_ops = [Buf() for _ in range(2)]

    mixT = c.sb([128, KC, TT], BF16, "mixT")
    b_mix = Buf()
    accf = c.sb([128, 3, TT], F32, "accf")
    b_accf = Buf()
    accr = c.sb([128, 3, TT], F32, "accr")
    b_accr = Buf()
    stg = [c.sb([128, 3, TT], F32, "stg0")] * 2
    b_stg = [Buf()] * 2
    fs4 = c.sb([6, 4, TT], F32, "fs4")
    b_fs4 = Buf()
    rs6 = c.sb([6, TT], F32, "rs6")
    b_rs6 = Buf()
    rgt = c.sb([128, 3, TT], BF16, "rgt")
    b_rgt = Buf()
    sq3 = c.sb([128, 3, TT], BF16, "sq3")
    b_sq3 = Buf()
    rr = c.sb([128, TT], F32, "rr")
    b_rr = Buf()
    stc = [0]

    def combine(X, bX, w, x_src, fo_src, fs_src, ro_src, oa_src, rg_src):
        c.dma("sp", X[:, :, 0:w], x_src.rearrange("(k p) t -> p k t", p=128), (), [bX])
        c.dma("sp", mixT[:, 0:2, 0:w], oa_src.rearrange("(k p) t -> p k t", p=128), (), [b_mix])
        c.dma("sp", rgt[:, :, 0:w], rg_src.rearrange("(k p) t -> p k t", p=128), (), [b_rgt])
        c.dma("sp", fs4[:, :, 0:w], fs_src.rearrange("r h t -> h r t"), (), [b_fs4])
        for (src, acc, b_acc) in ((fo_src, accf, b_accf), (ro_src, accr, b_accr)):
            c.dma("sp", acc[:, :, 0:w], src[0].rearrange("(k p) t -> p k t", p=128), (), [b_acc])
            for r in range(1, 4):
                i = stc[0] % 2
                stc[0] += 1
                c.dma("sp", stg[i][:, :, 0:w], src[r].rearrange("(k p) t -> p k t", p=128), (), [b_stg[i]])
                c.tt("pool" if r == 2 else "dve", acc[:, :, 0:w], acc[:, :, 0:w], stg[i][:, :, 0:w], ALU.add,
                     [b_acc, b_stg[i]], [b_acc])
        c.tt("dve", rs6[:, 0:w], fs4[:, 0, 0:w], fs4[:, 1, 0:w], ALU.add, [b_fs4], [b_rs6])
        c.tt("dve", rs6[:, 0:w], rs6[:, 0:w], fs4[:, 2, 0:w], ALU.add, [b_fs4, b_rs6], [b_rs6])
        c.tt("dve", rs6[:, 0:w], rs6[:, 0:w], fs4[:, 3, 0:w], ALU.add, [b_fs4, b_rs6], [b_rs6])
        c.recip(rs6[:, 0:w], rs6[:, 0:w], [b_rs6], [b_rs6])
        for j in range(3):
            i = j % 2
            c.mm(gps[i][:, 0:w], selb[:, j, :], rs6[:, 0:w], True, True, [b_selb, b_rs6], [b_gps[i]])
            c.tt("dve", mixT[:, 5 + j, 0:w], accf[:, j, 0:w], gps[i][:, 0:w], ALU.mult, [b_accf, b_gps[i]], [b_mix])
        c.act(sq3[:, :, 0:w], accr[:, :, 0:w], AF.Square, [b_accr], [b_sq3])
        for j in range(3):
            i = j % 2
            c.mm(ups[i][:, 0:w], blk[:], sq3[:, j, 0:w], True, True, [b_blk, b_sq3], [b_ups[i]])
            c.act(rr[:, 0:w], ups[i][:, 0:w], AF.Sqrt, [b_ups[i], cm.b_eps], [b_rr], bias=cm.epsb[:], scale=1.0 / 64)
            c.recip(rr[:, 0:w], rr[:, 0:w], [b_rr], [b_rr])
            c.tt("dve", rr[:, 0:w], rr[:, 0:w], accr[:, j, 0:w], ALU.mult, [b_rr, b_accr], [b_rr])
            c.tt("pool", mixT[:, 2 + j, 0:w], rr[:, 0:w], rgt[:, j, 0:w], ALU.mult, [b_rr, b_rgt], [b_mix])
        for o in range(KC):
            P, bP = ops_[o % 2], b_ops[o % 2]
            for k in range(KC):
                c.mm(P[:, 0:w], wo_t[:, o, k * 128:(k + 1) * 128], mixT[:, k, 0:w], k == 0, k == KC - 1,
                     [b_wo, b_mix], [bP])
            c.copy("act", fT[:, o, 0:w], P[:, 0:w], [bP], [b_fT])
        cm.rstd(fT[:, :, 0:w], b_fT, rs[:, 0:w], b_rs, width=w)
        for k in range(KC):
            c.stt("dve", tmp[:, 0:w], fT[:, k, 0:w], gmpost[:, k:k + 1], rs[:, 0:w], ALU.mult, ALU.mult,
                  [b_fT, b_gmpost, b_rs], [b_tmp])
            c.tt("pool", X[:, k, 0:w], X[:, k, 0:w], tmp[:, 0:w], ALU.add, [bX, b_tmp], [bX])

    def make_h(src, b_src, gain, b_gain, dst, b_dst, rs_, b_rs_, width):
        for k in range(KC):
            c.stt("dve", dst[:, k, 0:width], src[:, k, 0:width], gain[:, k:k + 1], rs_[:, 0:width],
                  ALU.mult, ALU.mult, [b_src, b_gain, b_rs_], [b_dst])

    wcount = [0]
    for t in range(ntile):
        X, bX = xt[t % 2], b_xt[t % 2]
        first_in_piece = (t * TT) % PT == 0
        piece = (t * TT) // PT
        cols = slice(t * TT, (t + 1) * TT)
        c.dma("pool", pt[:], pT[:, cols].rearrange("(k p) t -> p k t", p=128), (), [b_pt])
        if first_in_piece:
            combine(hx, b_hx, 2, h_x, h_fo, h_fs, h_ro, h_oa, h_rg)
            cm.rstd(hx[:], b_hx, hrs[:], b_hrs, width=2)
            make_h(hx, b_hx, gpre, b_gpre, hh, b_hh, hrs, b_hrs, 2)
        combine(X, bX, TT, xT_in[:, cols], po_fo[:, :, cols], po_fs[:, :, cols], po_ro[:, :, cols],
                oaT[:, cols], rgT[:, cols])
        cm.rstd(X[:], bX, rs[:], b_rs)
        make_h(X, bX, gpre, b_gpre, hT, b_hT, rs, b_rs, TT)
        for f in range(FC):
            w = wcount[0] % NW
            wcount[0] += 1
            c.dma("sp", wgt[w][:], wg_s[f], [b_wgs[f]], [b_wgt[w]])
            c.dma("sp", wut[w][:], wu_s[f], [b_wus[f]], [b_wut[w]])
            i = f % 2
            if first_in_piece:
                for k in range(KC):
                    c.mm(gps[i][:, 0:2], wgt[w][:, k * 128:(k + 1) * 128], hh[:, k, :], k == 0, k == KC - 1,
                         [b_wgt[w], b_hh], [b_gps[i]])
                c.copy("act", gcar[:, f, :], gps[i][:, 0:2], [b_gps[i]], [b_gcar[f]])
            for k in range(KC):
                c.mm(gps[i][:], wgt[w][:, k * 128:(k + 1) * 128], hT[:, k, :], k == 0, k == KC - 1,
                     [b_wgt[w], b_hT], [b_gps[i]])
            for k in range(KC):
                c.mm(ups[i][:], wut[w][:, k * 128:(k + 1) * 128], hT[:, k, :], k == 0, k == KC - 1,
                     [b_wut[w], b_hT], [b_ups[i]])
            G, bG = gs[i], b_gs[i]
            c.copy("pool", G[:, 0:2], gcar[:, f, :], [b_gcar[f]], [bG])
            c.copy("act", G[:, 2:TT + 2], gps[i][:], [b_gps[i]], [bG])
            c.copy("pool", gcar[:, f, :], G[:, TT:TT + 2], [bG], [b_gcar[f]])
            c.act(c1[i][:], G[:, 2:TT + 2], AF.Identity, [bG, b_cw, b_cb], [b_c1[i]],
                  bias=cb[:, f:f + 1], scale=cw[:, f, 2:3])
            c.stt("dve", c2[i][:], G[:, 1:TT + 1], cw[:, f, 1:2], c1[i][:], ALU.mult, ALU.add,
                  [bG, b_cw, b_c1[i]], [b_c2[i]])
            c.stt("dve", c1[i][:], G[:, 0:TT], cw[:, f, 0:1], c2[i][:], ALU.mult, ALU.add,
                  [bG, b_cw, b_c2[i]], [b_c1[i]])
            c.act(c2[i][:], c1[i][:], AF.Gelu_apprx_tanh, [b_c1[i]], [b_c2[i]])
            c.tt("dve", actT[:, f, :], c2[i][:], ups[i][:], ALU.mult, [b_c2[i], b_ups[i]], [b_act[f]])
        for o in range(KC):
            w = o % 2
            c.dma("sp", wdt[w][:], wd_s[o], [b_wds[o]], [b_wdt[w]])
            P, bP = ops_[o % 2], b_ops[o % 2]
            for f in range(FC):
                c.mm(P[:], wdt[w][:, f * 128:(f + 1) * 128], actT[:, f, :], f == 0, f == FC - 1,
                     [b_wdt[w], b_act[f]], [bP])
            c.copy("act", fT[:, o, :], P[:], [bP], [b_fT])
        cm.rstd(fT[:], b_fT, rs[:], b_rs)
        for k in range(KC):
            c.stt("dve", tmp[:], fT[:, k, :], gpost[:, k:k + 1], rs[:], ALU.mult, ALU.mult,
                  [b_fT, b_gpost, b_rs], [b_tmp])
            c.tt("pool", X[:, k, :], X[:, k, :], tmp[:], ALU.add, [bX, b_tmp], [bX])
        cm.rstd(X[:], bX, rs[:], b_rs)
        make_h(X, bX, gppre, b_gppre, hT, b_hT, rs, b_rs, TT)
        for o in range(KC):
            i = o % 2
            for k in range(KC):
                c.mm(gps[i][:], wpg_t[:, o, k * 128:(k + 1) * 128], hT[:, k, :], k == 0, k == KC - 1,
                     [b_wpg, b_hT], [b_gps[i]])
            for k in range(2):
                c.mm(ups[i][:], wpp_t[:, o, k * 128:(k + 1) * 128], pt[:, k, :], k == 0, k == 1,
                     [b_wpp, b_pt], [b_ups[i]])
            c.act(c1[i][:], gps[i][:], AF.Sigmoid, [b_gps[i]], [b_c1[i]])
            c.tt("dve", fT[:, o, :], c1[i][:], ups[i][:], ALU.mult, [b_c1[i], b_ups[i]], [b_fT])
        cm.rstd(fT[:], b_fT, rs[:], b_rs)
        for k in range(KC):
            c.stt("dve", tmp[:], fT[:, k, :], gppost[:, k:k + 1], rs[:], ALU.mult, ALU.mult,
                  [b_fT, b_gppost, b_rs], [b_tmp])
            c.tt("pool", X[:, k, :], X[:, k, :], tmp[:], ALU.add, [bX, b_tmp], [bX])
        c.dma("sp", outT[:, t * TT:(t + 1) * TT].rearrange("(k p) t -> p k t", p=128), X[:], [bX], (), final=True)
    return c.finish()


NIN = 3206
C_AU, C_AV, C_RQ, C_RK, C_RV, C_RG, C_FQ, C_FK, C_FV, C_FF = 0, 256, 512, 896, 1280, 1664, 2048, 2432, 2816, 3200
C_RH = 3208
NWC = C_RH + 768
TWO_PI = 2.0 * math.pi


def build_A(NT, NP):
    c = Ctx("A")
    PT = NT // NP
    ntile = NT // TT
    xT = c.din("xT", [D, NT], F32)
    posb = c.din("posb", [128, NT], I32)
    w_in = c.din("w_in", [D, NIN], F32)
    g_pre = c.din("mix_pre_g", [128, KC], F32)
    vgb = c.din("sgu_vg_b", [128, 256], F32)
    swT = c.din("sgu_wT", [4, 128, 128], F32)
    sbias = c.din("sgu_b", [2, 2, 128], F32)
    fbf = c.din("fox_bf", [6, 1], F32)
    invf_d = c.din("invf", [128, 1], F32)
    gq_d = c.din("gq", [3, 128, TT], F32)
    gk_d = c.din("gk", [3, 128, TT], F32)
    tri_d = c.din("tri", [128, 128], F32)
    e2_d = c.din("e2", [2, 128], F32)
    o_aT = c.dout("out_aT", [256, NT], BF16)
    o_rq = c.dout("rqT", [384, NT], BF16)
    o_rk = c.dout("rkT", [384, NT], BF16)
    o_rg = c.dout("rgT", [384, NT], BF16)
    o_fq = c.dout("fqT", [384, NT], BF16)
    o_fk = c.dout("fkT", [384, NT], BF16)
    o_v = c.dout("vtok", [NT, 768], BF16)
    o_cl = c.dout("cl", [6, NT], F32)

    cm = Common(c)
    gpre, b_gpre = load_vec_fm(c, "sp", g_pre, KC, "gpre")
    W = c.sb([128, KC, NWC], BF16, "W")
    b_W = Buf()
    for k in range(KC):
        c.dma("pool", W[:, k, 0:NIN], w_in[k * 128:(k + 1) * 128, :], (), [b_W])
    for k in range(KC):
        src = W[:, k, C_RQ:C_RQ + 768].rearrange("p (h d) -> p h d", d=64)
        dst = W[:, k, C_RH:C_RH + 768].rearrange("p (h d) -> p h d", d=64)
        c.act(dst[:, :, 0:32], src[:, :, 32:64], AF.Copy, [b_W], [b_W], scale=-1.0)
        c.copy("dve", dst[:, :, 32:64], src[:, :, 0:32], [b_W], [b_W])
    def ld(name, shape, src, q="sp", dt=F32):
        t = c.sb(shape, dt, name + "_sb")
        b = Buf()
        c.dma(q, t[:], src, (), [b])
        return t, b
    invf, b_invf = ld("invf", [128, 1], invf_d)
    gq, b_gq = ld("gq", [128, 3, TT], gq_d.rearrange("j p t -> p j t"))
    gk, b_gk = ld("gk", [128, 3, TT], gk_d.rearrange("j p t -> p j t"))
    tri, b_tri = ld("tri", [128, 128], tri_d)
    e2, b_e2 = ld("e2", [2, 128], e2_d, q="pool", dt=BF16)
    vg_b, b_vgb = ld("vgb", [128, 256], vgb)
    fb, b_fb = ld("fb", [6, 1], fbf)
    nfb = c.sb([6, 1], F32, "nfb")
    b_nfb = Buf()
    c.ts("dve", nfb[:], fb[:], -1.0, None, ALU.mult, None, [b_fb], [b_nfb])
    mpi = c.sb([128, 1], F32, "mpi")
    b_mpi = Buf()
    c.memset("pool", mpi[:], -math.pi, [b_mpi])
    one6 = c.sb([6, 1], F32, "one6")
    b_one6 = Buf()
    c.memset("pool", one6[:], 1.0, [b_one6])
    ones_row = c.sb([6, TT], F32, "ones_row")
    b_onesr = Buf()
    c.memset("pool", ones_row[:], 1.0, [b_onesr])
    swf = c.sb([128, 4, 128], F32, "swf")
    b_swf = Buf()
    c.dma("sp", swf[:], swT.rearrange("h s t -> s h t"), (), [b_swf])
    swm = c.sb([128, 4, 128], BF16, "swm")
    b_swm = Buf()
    for h in range(4):
        c.tt("dve", swm[:, h, :], swf[:, h, :], tri[:], ALU.mult, [b_swf, b_tri], [b_swm])
    bf_ = c.sb([2, 2, 128], F32, "sbf")
    b_bf = Buf()
    c.dma("sp", bf_[:], sbias.rearrange("j h t -> h j t"), (), [b_bf])
    bhi = c.sb([2, 2, 128], BF16, "bhi")
    bback = c.sb([2, 2, 128], F32, "bback")
    blo = c.sb([2, 2, 128], BF16, "blo")
    b_bhi, b_bback, b_blo = Buf(), Buf(), Buf()
    c.copy("dve", bhi[:], bf_[:], [b_bf], [b_bhi])
    c.copy("dve", bback[:], bhi[:], [b_bhi], [b_bback])
    c.tt("dve", blo[:], bf_[:], bback[:], ALU.subtract, [b_bf, b_bback], [b_blo])

    xt = [c.sb([128, KC, TT], F32, "xt%d" % i) for i in range(2)]
    b_xt = [Buf() for _ in range(2)]
    hT = c.sb([128, KC, TT], BF16, "hT")
    b_hT = Buf()
    rs = c.sb([128, TT], F32, "rs")
    b_rs = Buf()
    posi = c.sb([128, TT], I32, "posi")
    b_posi = Buf()
    posf = c.sb([128, TT], F32, "posf")
    b_posf = Buf()
    ang = c.sb([128, TT], F32, "ang")
    b_ang = Buf()
    kint = c.sb([128, TT], I32, "kint")
    b_kint = Buf()
    ang2 = c.sb([128, TT], F32, "ang2")
    b_ang2 = Buf()
    sinT = c.sb([128, TT], F32, "sinT")
    b_sin = Buf()
    cosT = c.sb([128, TT], F32, "cosT")
    b_cos = Buf()
    NF = 3
    fps = [c.ps(name="fps%d" % i) for i in range(NF)]
    b_fps = [Buf() for _ in range(NF)]
    vps = [c.ps(name="vps%d" % i) for i in range(2)]
    b_vps = [Buf() for _ in range(2)]
    sps = [c.ps(name="sps%d" % i) for i in range(2)]
    b_sps = [Buf() for _ in range(2)]
    NOB = 4
    ob = [c.sb([128, TT], BF16, "ob%d" % i) for i in range(NOB)]
    b_ob = [Buf() for _ in range(NOB)]
    guT = [c.sb([128, TT], F32, "guT%d" % i) for i in range(2)]
    b_gu = [Buf() for _ in range(2)]
    vgt = c.sb([128, 256], F32, "vgt")
    b_vgt = Buf()
    sqv = c.sb([128, 256], F32, "sqv")
    b_sqv = Buf()
    ssv = c.sb([128, 4], F32, "ssv")
    b_ssv = Buf()
    rtv = c.sb([128, 4], F32, "rtv")
    b_rtv = Buf()
    vpad = [c.sb([128, 4, 128], BF16, "vpad%d" % i) for i in range(2)]
    b_vpad = [Buf() for _ in range(2)]
    for i in range(2):
        c.memset("pool", vpad[i][:], 0.0, [b_vpad[i]])
    vout = [c.sb([128, 768], BF16, "vout%d" % i) for i in range(2)]
    b_vout = [Buf() for _ in range(2)]
    t1 = c.sb([128, TT], F32, "t1")
    b_t1 = Buf()
    t2 = c.sb([128, TT], F32, "t2")
    b_t2 = Buf()
    t3 = c.sb([128, TT], F32, "t3")
    b_t3 = Buf()
    fe = c.sb([6, TT], F32, "fe")
    b_fe = Buf()
    fcs = [c.sb([6, TT], F32, "fcs%d" % i) for i in range(2)]
    b_fcs = [Buf() for _ in range(2)]
    fneg = c.sb([6, TT], F32, "fneg")
    b_fneg = Buf()

    cnt = {"f": 0, "o": 0, "sb": 0}

    def next_f():
        i = cnt["f"] % NF
        cnt["f"] += 1
        return fps[i], b_fps[i]

    def next_o():
        i = cnt["o"] % NOB
        cnt["o"] += 1
        return ob[i], b_ob[i]

    def fm_mm(P, bP, col0, ncol=128):
        for k in range(KC):
            c.mm(P[0:ncol, :], W[:, k, col0:col0 + ncol], hT[:, k, :], k == 0, k == KC - 1, [b_W, b_hT], [bP])

    for t in range(ntile):
        X, bX = xt[t % 2], b_xt[t % 2]
        cols = slice(t * TT, (t + 1) * TT)
        first_in_piece = (t * TT) % PT == 0
        c.dma("sp", X[:], xT[:, cols].rearrange("(k p) t -> p k t", p=128), (), [bX])
        c.dma("sp", posi[:], posb[:, cols], (), [b_posi])
        cm.rstd(X[:], bX, rs[:], b_rs)
        for k in range(KC):
            c.stt("dve", hT[:, k, :], X[:, k, :], gpre[:, k:k + 1], rs[:], ALU.mult, ALU.mult,
                  [bX, b_gpre, b_rs], [b_hT])
        c.copy("pool", posf[:], posi[:], [b_posi], [b_posf])
        C1 = 6.28125
        C2 = TWO_PI - C1
        for (shift, dstT, b_dst) in ((0.0, sinT, b_sin), (0.5 * math.pi, cosT, b_cos)):
            c.ts("dve", ang[:], posf[:], invf[:, 0:1], shift, ALU.mult, ALU.add, [b_posf, b_invf], [b_ang])
            c.ts("dve", ang2[:], ang[:], 1.0 / TWO_PI, None, ALU.mult, None, [b_ang], [b_ang2])
            c.copy("dve", kint[:], ang2[:], [b_ang2], [b_kint])
            c.copy("dve", ang2[:], kint[:], [b_kint], [b_ang2])
            c.stt("dve", ang[:], ang2[:], -C1, ang[:], ALU.mult, ALU.add, [b_ang2, b_ang], [b_ang])
            c.stt("dve", ang[:], ang2[:], -C2, ang[:], ALU.mult, ALU.add, [b_ang2, b_ang], [b_ang])
            c.ts("dve", ang[:], ang[:], -math.pi, math.pi, ALU.max, ALU.min, [b_ang], [b_ang])
            c.act(dstT[:], ang[:], AF.Sin, [b_ang], [b_dst])
        for sbk in range(4):
            tok = slice(sbk * 128, (sbk + 1) * 128)
            g0, bg0 = vps[0], b_vps[0]
            g1, bg1 = vps[1], b_vps[1]
            for k in range(KC):
                c.mm(g0[:, 0:256], hT[:, k, tok], W[:, k, C_AV:C_AV + 256], k == 0, k == KC - 1, [b_hT, b_W], [bg0])
            for k in range(KC):
                c.mm(g0[:, 256:512], hT[:, k, tok], W[:, k, C_RV:C_RV + 256], k == 0, k == KC - 1,
                     [b_hT, b_W], [bg0])
            for k in range(KC):
                c.mm(g1[:, 0:128], hT[:, k, tok], W[:, k, C_RV + 256:C_RV + 384], k == 0, k == KC - 1,
                     [b_hT, b_W], [bg1])
            for k in range(KC):
                c.mm(g1[:, 128:512], hT[:, k, tok], W[:, k, C_FV:C_FV + 384], k == 0, k == KC - 1,
                     [b_hT, b_W], [bg1])
            i = cnt["sb"] % 2
            cnt["sb"] += 1
            VO, bVO = vout[i], b_vout[i]
            c.copy("dve", VO[:, 0:256], g0[:, 256:512], [bg0], [bVO])
            c.copy("act", VO[:, 256:768], g1[:, 0:512], [bg1], [bVO])
            c.dma("sp", o_v[t * TT + sbk * 128:t * TT + (sbk + 1) * 128, :], VO[:], [bVO], (), final=True)
            c.act(vgt[:], g0[:, 0:256], AF.Gelu_apprx_tanh, [bg0], [b_vgt])
            c.tt("pool", sqv[:], vgt[:], vgt[:], ALU.mult, [b_vgt], [b_sqv])
            c.s.op("dve", lambda e: e.tensor_reduce(out=ssv[:], in_=sqv[:].rearrange("p (h d) -> p h d", d=64),
                                                    axis=AX.X, op=ALU.add), [b_sqv], [b_ssv])
            c.act(rtv[:], ssv[:], AF.Sqrt, [b_ssv, cm.b_eps], [b_rtv], bias=cm.epsb[:], scale=1.0 / 64)
            c.recip(ssv[:], rtv[:], [b_rtv], [b_ssv])
            VP, bVP = vpad[i], b_vpad[i]
            for h in range(4):
                off = 64 * (h % 2)
                c.stt("dve", VP[:, h, off:off + 64], vgt[:, h * 64:(h + 1) * 64], ssv[:, h:h + 1],
                      vg_b[:, h * 64:(h + 1) * 64], ALU.mult, ALU.mult, [b_vgt, b_ssv, b_vgb], [bVP])
            for j in range(2):
                so = sps[j][:, tok]
                c.mm(so, VP[:, 2 * j, :], swm[:, 2 * j, :], True, False, [bVP, b_swm], [b_sps[j]])
                c.mm(so, VP[:, 2 * j + 1, :], swm[:, 2 * j + 1, :], False, False, [bVP, b_swm], [b_sps[j]])
                c.mm(so, e2[:], bhi[:, j, :], False, False, [b_e2, b_bhi], [b_sps[j]])
                c.mm(so, e2[:], blo[:, j, :], False, True, [b_e2, b_blo], [b_sps[j]])
        for j in range(2):
            P, bP = next_f()
            fm_mm(P, bP, C_AU + 128 * j)
            c.act(guT[j][:], P[:], AF.Gelu_apprx_tanh, [bP], [b_gu[j]])
            O, bO = next_o()
            c.tt("dve", O[:], guT[j][:], sps[j][:], ALU.mult, [b_gu[j], b_sps[j]], [bO])
            c.dma("sp", o_aT[j * 128:(j + 1) * 128, cols], O[:], [bO], (), final=True)
        for (c0, rh0, G, bG, dst) in ((C_RQ, C_RH, gq, b_gq, o_rq), (C_RK, C_RH + 384, gk, b_gk, o_rk)):
            for j in range(3):
                P, bP = next_f()
                fm_mm(P, bP, c0 + 128 * j)
                P2, bP2 = next_f()
                fm_mm(P2, bP2, rh0 + 128 * j)
                c.tt("dve", t1[:], P[:], cosT[:], ALU.mult, [bP, b_cos], [b_t1])
                c.copy("act", t2[:], P2[:], [bP2], [b_t2])
                c.tt("pool", t3[:], t2[:], sinT[:], ALU.mult, [b_t2, b_sin], [b_t3])
                c.tt("pool", t1[:], t1[:], t3[:], ALU.add, [b_t1, b_t3], [b_t1])
                O, bO = next_o()
                c.tt("pool", O[:], t1[:], G[:, j, :], ALU.mult, [b_t1, bG], [bO])
                c.dma("sp", dst[j * 128:(j + 1) * 128, cols], O[:], [bO], (), final=True)
        for (c0, dst, func, scale) in ((C_RG, o_rg, AF.Silu, None), (C_FQ, o_fq, AF.Copy, 0.125),
                                        (C_FK, o_fk, AF.Copy, None)):
            for j in range(3):
                P, bP = next_f()
                fm_mm(P, bP, c0 + 128 * j)
                O, bO = next_o()
                c.act(O[:], P[:], func, [bP], [bO], scale=scale)
                c.dma("sp", dst[j * 128:(j + 1) * 128, cols], O[:], [bO], (), final=True)
        P, bP = next_f()
        fm_mm(P, bP, C_FF, 6)
        c.act(fe[:], P[0:6, :], AF.Exp, [bP, b_nfb], [b_fe], bias=nfb[:], scale=-1.0)
        c.act(fe[:], fe[:], AF.Ln, [b_fe, b_one6], [b_fe], bias=one6[:])
        FC_, bFC = fcs[t % 2], b_fcs[t % 2]
        prev, bprev = fcs[(t + 1) % 2], b_fcs[(t + 1) % 2]
        if first_in_piece:
            c.s.op("dve", lambda e, FC_=FC_: e.tensor_tensor_scan(out=FC_[:], data0=ones_row[:], data1=fe[:],
                                                                  initial=0.0, op0=ALU.mult, op1=ALU.add),
                   [b_onesr, b_fe], [bFC])
        else:
            c.s.op("dve", lambda e, FC_=FC_, prev=prev: e.tensor_tensor_scan(
                out=FC_[:], data0=ones_row[:], data1=fe[:], initial=prev[:, TT - 1:TT], op0=ALU.mult, op1=ALU.add),
                [b_onesr, b_fe, bprev], [bFC])
        c.ts("dve", fneg[:], FC_[:], -1.0, None, ALU.mult, None, [bFC], [b_fneg])
        c.dma("sp", o_cl[:, cols], fneg[:], [b_fneg], (), final=True)
    return c.finish()


RET_LOOKBACK_TILES = [2, 3, 6, 11, 21, 1 << 30]


def build_B(S):
    c = Ctx("B")
    NQ = S // TT
    SK = S // 4
    QPQ = NQ // 4
    fqT = c.din("fqT", [6, 64, S], BF16)
    rqT = c.din("rqT", [6, 64, S], BF16)
    fkT = c.din("fkT", [6, 64, SK], BF16)
    rkT = c.din("rkT", [6, 64, SK], BF16)
    fv = c.din("fv", [6, 128, NQ, 64], BF16)
    rv = c.din("rv", [6, 128, NQ, 64], BF16)
    cl_q = c.din("cl_q", [6, S], F32)
    cl_k = c.din("cl_k", [6, SK], F32)
    tots = c.din("tots", [6, 4], F32)
    dmask_d = c.din("dmask", [128, TT], F32)
    po_fo = c.dout("po_fo", [384, S], F32)
    po_fs = c.dout("po_fs", [6, S], F32)
    po_ro = c.dout("po_ro", [384, S], F32)
    cqp = c.dscratch("cqp", [6, 3, S], BF16)
    ckn = c.dscratch("ckn", [6, 3, SK], BF16)
    b_cqp, b_ckn = Buf(), Buf()

    nm_d = c.din("negmask", [128, TT], F32)
    id_d = c.din("ident", [128, 128], F32)
    negm = c.sb([128, TT], BF16, "negm")
    b_negm = Buf()
    c.dma("pool", negm[:], nm_d, (), [b_negm])
    ident = c.sb([128, 128], BF16, "ident_sb")
    b_ident = Buf()
    c.dma("pool", ident[:], id_d, (), [b_ident])
    dmf = c.sb([128, TT], F32, "dmf")
    b_dmf = Buf()
    c.dma("sp", dmf[:], dmask_d, (), [b_dmf])
    dm = c.sb([128, TT], BF16, "dm")
    b_dm = Buf()
    c.copy("dve", dm[:], dmf[:], [b_dmf], [b_dm])
    tt_ = c.sb([6, 4], F32, "tots_sb")
    b_tt = Buf()
    c.dma("sp", tt_[:], tots, (), [b_tt])
    off = c.sb([6, 4], F32, "off")
    b_off = Buf()
    c.memset("dve", off[:, 0:1], 0.0, [b_off])
    for q in range(1, 4):
        c.tt("dve", off[:, q:q + 1], off[:, q - 1:q], tt_[:, q - 1:q], ALU.add, [b_off, b_tt], [b_off])
    CW = 2048
    cc = c.sb([6, CW], F32, "cc")
    b_cc = Buf()
    cb16 = c.sb([6, 3, CW], BF16, "cb16")
    b_cb16 = Buf()
    cback = c.sb([6, CW], F32, "cback")
    b_cback = Buf()

    def split3(src_d, n, width_per_quarter, dst, b_dst, sign):
        for q in range(4):
            for w0 in range(0, width_per_quarter, CW):
                w = min(CW, width_per_quarter - w0)
                col0 = q * width_per_quarter + w0
                c.dma("sp", cc[:, 0:w], src_d[:, col0:col0 + w], (), [b_cc])
                c.ts("dve", cc[:, 0:w], cc[:, 0:w], off[:, q:q + 1], sign, ALU.add, ALU.mult, [b_cc, b_off], [b_cc])
                for part in range(3):
                    c.copy("dve", cb16[:, part, 0:w], cc[:, 0:w], [b_cc], [b_cb16])
                    if part < 2:
                        c.copy("dve", cback[:, 0:w], cb16[:, part, 0:w], [b_cb16], [b_cback])
                        c.tt("dve", cc[:, 0:w], cc[:, 0:w], cback[:, 0:w], ALU.subtract, [b_cc, b_cback], [b_cc])
                c.dma("sp", dst[:, :, col0:col0 + w], cb16[:, :, 0:w], [b_cb16], [b_dst])

    split3(cl_q, S, S // 4, cqp, b_cqp, 1.0)
    split3(cl_k, SK, SK // 4, ckn, b_ckn, -1.0)

    KE = [c.sb([70, SK], BF16, "KE%d" % i) for i in range(2)]
    b_KE = [Buf() for _ in range(2)]
    VE = [c.sb([128, NQ, 65], BF16, "VE%d" % i) for i in range(2)]
    b_VE = [Buf() for _ in range(2)]
    for i in range(2):
        c.memset("pool", KE[i][64:70, :], 1.0, [b_KE[i]])
        c.memset("pool", VE[i][:, :, 64:65], 1.0, [b_VE[i]])
    NQB = 3
    QE = [c.sb([70, TT], BF16, "QE%d" % i) for i in range(NQB)]
    b_QE = [Buf() for _ in range(NQB)]
    for i in range(NQB):
        c.memset("pool", QE[i][64:70, :], 1.0, [b_QE[i]])
    NPB = 4
    PB = [c.sb([128, TT], BF16, "PB%d" % i) for i in range(NPB)]
    b_PB = [Buf() for _ in range(NPB)]
    NS = 3
    sp_ = [c.ps(name="sps%d" % i) for i in range(NS)]
    b_sp = [Buf() for _ in range(NS)]
    op_ = [c.ps(name="ops%d" % i) for i in range(2)]
    b_op = [Buf() for _ in range(2)]
    osb = [c.sb([65, TT], F32, "osb%d" % i) for i in range(2)]
    b_osb = [Buf() for _ in range(2)]
    cn = {"q": 0, "p": 0, "s": 0, "o": 0, "kv": 0}

    for typ in ("fox", "ret"):
        for h in range(6):
            kv = cn["kv"] % 2
            cn["kv"] += 1
            K_, bK = KE[kv], b_KE[kv]
            V_, bV = VE[kv], b_VE[kv]
            if typ == "fox":
                c.dma("sp", K_[0:64, :], fkT[h], (), [bK])
                c.dma("sp", K_[64:67, :], ckn[h], [b_ckn], [bK])
                c.dma("sp", V_[:, :, 0:64], fv[h], (), [bV])
                rows, vcols = 70, 65
            else:
                c.dma("sp", K_[0:64, :], rkT[h], (), [bK])
                c.dma("sp", V_[:, :, 0:64], rv[h], (), [bV])
                rows, vcols = 64, 64
            for i in range(NQ):
                qi = cn["q"] % NQB
                cn["q"] += 1
                Q_, bQ = QE[qi], b_QE[qi]
                cols = slice(i * TT, (i + 1) * TT)
                if typ == "fox":
                    c.dma("sp", Q_[0:64, :], fqT[h][:, cols], (), [bQ])
                    c.dma("sp", Q_[67:70, :], cqp[h][:, cols], [b_cqp], [bQ])
                    lo = 0
                else:
                    c.dma("sp", Q_[0:64, :], rqT[h][:, cols], (), [bQ])
                    lo = max(0, i - RET_LOOKBACK_TILES[h])
                oi = cn["o"] % 2
                cn["o"] += 1
                O_, bO = op_[oi], b_op[oi]
                for m in range(lo, i + 1):
                    si = cn["s"] % NS
                    cn["s"] += 1
                    S_, bS = sp_[si], b_sp[si]
                    pi = cn["p"] % NPB
                    cn["p"] += 1
                    P_, bP = PB[pi], b_PB[pi]
                    diag = (m == i)
                    if typ == "fox":
                        c.mm(S_[:], K_[0:rows, m * 128:(m + 1) * 128], Q_[0:rows, :], True, not diag, [bK, bQ], [bS])
                        if diag:
                            c.mm(S_[:], ident[:], negm[:], False, True, [b_ident, b_negm], [bS])
                        c.act(P_[:], S_[:], AF.Exp, [bS], [bP])
                    else:
                        c.mm(S_[:], K_[0:rows, m * 128:(m + 1) * 128], Q_[0:rows, :], True, True, [bK, bQ], [bS])
                        cst = math.exp(LOG_G[h] * (512.0 * (i - m) - 127.0))
                        c.ts("dve", P_[:], S_[:], cst, None, ALU.mult, None, [bS], [bP])
                        if diag:
                            c.tt("pool", P_[:], P_[:], dm[:], ALU.mult, [bP, b_dm], [bP])
                    c.mm(O_[0:vcols, :], V_[:, m, 0:vcols], P_[:], m == lo, m == i, [bV, bP], [bO])
                OS, bOS = osb[oi], b_osb[oi]
                c.copy("act" if typ == "ret" else "dve", OS[0:vcols, :], O_[0:vcols, :], [bO], [bOS])
                if typ == "fox":
                    c.dma("sp", po_fo[h * 64:(h + 1) * 64, cols], OS[0:64, :], [bOS], (), final=True)
                    c.dma("sp", po_fs[h:h + 1, cols], OS[64:65, :], [bOS], (), final=True)
                else:
                    c.dma("sp", po_ro[h * 64:(h + 1) * 64, cols], OS[0:64, :], [bOS], (), final=True)
    return c.finish()


LOG_G = [math.log(1.0 - 2.0 ** (-5.0 - h)) for h in range(6)]


def make_consts():
    p = np.arange(128)
    invf = (10000.0 ** (-((p % 64) % 32) / 32.0)).astype(np.float32).reshape(128, 1)
    t = np.arange(TT)
    gq = np.zeros((3, 128, TT), np.float64)
    gk = np.zeros((3, 128, TT), np.float64)
    for j in range(3):
        for half in range(2):
            lg = LOG_G[2 * j + half]
            gq[j, half * 64:(half + 1) * 64, :] = np.exp(lg * t)[None, :]
            gk[j, half * 64:(half + 1) * 64, :] = 0.125 * np.exp(lg * (127 - t))[None, :]
    tri = (p[None, :] >= p[:, None]).astype(np.float32)
    e2 = np.zeros((2, 128), np.float32)
    e2[0, :64] = 1.0
    e2[1, 64:] = 1.0
    return {"invf": invf, "gq": gq.astype(np.float32), "gk": gk.astype(np.float32), "tri": tri, "e2": e2}


_PROGS = {}


def _prog(name, *args):
    key = (name,) + args
    if key not in _PROGS:
        _PROGS[key] = {"A": build_A, "B": build_B, "C": build_C}[name](*args)
    return _PROGS[key]


def _fm(v):
    return np.ascontiguousarray(np.asarray(v, np.float32).reshape(-1, 128).T)


def _lay_w(w, nin_c, nout_c):
    return np.ascontiguousarray(np.asarray(w, np.float32).reshape(nin_c, 128, nout_c, 128).transpose(2, 1, 0, 3))


def _run(nc, in_maps):
    res = run_bass_kernel_spmd(nc, in_maps, core_ids=list(range(len(in_maps))))
    return res.results


def kernel(x, p, positions, mix_pre_g, w_in, sgu_v_g, sgu_w, sgu_b, fox_b_f, w_o, mix_post_g, ffn_pre_g,
           w_gate, w_up, conv_w, conv_b, w_down, ffn_post_g, ple_pre_g, w_ple_gate, w_ple_proj, ple_post_g):
    x = np.asarray(x, np.float32)
    p = np.asarray(p, np.float32)
    positions = np.asarray(positions, np.int32)
    B, S, _ = x.shape
    depth = p.shape[0]
    NT = S // 4
    NQ = S // TT
    ncore = B * 4
    consts = make_consts()
    pp = np.arange(128)
    selb = np.zeros((6, 3, 128), np.float32)
    for j in range(3):
        for q in range(128):
            selb[2 * j + q // 64, j, q] = 1.0
    blk = (pp[:, None] // 64 == pp[None, :] // 64).astype(np.float32)
    tq = np.arange(TT)
    dmask = [((128 * j + pp)[:, None] <= tq[None, :]).astype(np.float32) for j in range(4)]
    x_cur = x.copy()
    for i in range(depth):
        a_common = {
            "w_in": np.ascontiguousarray(np.asarray(w_in[i], np.float32)),
            "mix_pre_g": _fm(mix_pre_g[i]),
            "sgu_vg_b": np.ascontiguousarray(np.broadcast_to(np.asarray(sgu_v_g[i], np.float32).reshape(1, 256), (128, 256))),
            "sgu_wT": np.ascontiguousarray(np.asarray(sgu_w[i], np.float32).transpose(0, 2, 1)),
            "sgu_b": np.ascontiguousarray(np.asarray(sgu_b[i], np.float32).reshape(2, 2, 128)),
            "fox_bf": np.ascontiguousarray(np.asarray(fox_b_f[i], np.float32).reshape(6, 1)),
            "invf": consts["invf"], "gq": consts["gq"], "gk": consts["gk"], "tri": consts["tri"], "e2": consts["e2"],
        }
        xTs = []
        maps = []
        for b in range(B):
            for j in range(4):
                sl = slice(j * NT, (j + 1) * NT)
                xT = np.ascontiguousarray(x_cur[b, sl].T)
                xTs.append(xT)
                maps.append(dict(a_common, xT=xT,
                                 posb=np.ascontiguousarray(np.broadcast_to(positions[b, sl][None, :], (128, NT)))))
        rA = _run(_prog("A", NT, 1), maps)
        maps = []
        for b in range(B):
            cat = lambda k, ax: np.concatenate([np.asarray(rA[b * 4 + j][k]) for j in range(4)], axis=ax)
            fq, rq, fk, rk = cat("fqT", 1), cat("rqT", 1), cat("fkT", 1), cat("rkT", 1)
            vt = cat("vtok", 0)
            cl = cat("cl", 1)
            tots = np.ascontiguousarray(np.stack([np.asarray(rA[b * 4 + j]["cl"])[:, -1] for j in range(4)], axis=1))
            for j in range(4):
                selk = lambda a: np.ascontiguousarray(a.reshape(6, 64, NQ, 4, 128)[:, :, :, j, :].reshape(6, 64, S // 4))
                selv = lambda a: np.ascontiguousarray(a.reshape(NQ, 4, 128, 6, 64)[:, j].transpose(2, 1, 0, 3))
                maps.append({
                    "fqT": np.ascontiguousarray(fq.reshape(6, 64, S)), "rqT": np.ascontiguousarray(rq.reshape(6, 64, S)),
                    "fkT": selk(fk), "rkT": selk(rk),
                    "fv": selv(vt[:, 384:768]), "rv": selv(vt[:, 0:384]),
                    "cl_q": np.ascontiguousarray(cl),
                    "cl_k": np.ascontiguousarray(cl.reshape(6, NQ, 4, 128)[:, :, j, :].reshape(6, S // 4)),
                    "tots": tots, "dmask": dmask[j], "negmask": (dmask[j] - 1.0) * 30000.0,
                    "ident": np.eye(128, dtype=np.float32),
                })
        rB = _run(_prog("B", S), maps)
        c_common = {
            "w_o": _lay_w(w_o[i], 8, 8), "mix_post_g": _fm(mix_post_g[i]), "selb": selb, "blk": blk,
            "w_gate": _lay_w(w_gate[i], 8, 32), "w_up": _lay_w(w_up[i], 8, 32), "w_down": _lay_w(w_down[i], 32, 8),
            "w_ple_gate": _lay_w(w_ple_gate[i], 8, 8), "w_ple_proj": _lay_w(w_ple_proj[i], 2, 8),
            "conv_w": np.ascontiguousarray(np.asarray(conv_w[i], np.float32).T.reshape(32, 128, 3).transpose(1, 0, 2)),
            "conv_b": np.ascontiguousarray(np.asarray(conv_b[i], np.float32).reshape(32, 128).T),
            "ffn_pre_g": _fm(ffn_pre_g[i]), "ffn_post_g": _fm(ffn_post_g[i]),
            "ple_pre_g": _fm(ple_pre_g[i]), "ple_post_g": _fm(ple_post_g[i]),
        }
        maps = []
        for b in range(B):
            pf = [np.asarray(rB[b * 4 + r]["po_fo"]) for r in range(4)]
            ps_ = [np.asarray(rB[b * 4 + r]["po_fs"]) for r in range(4)]
            pr = [np.asarray(rB[b * 4 + r]["po_ro"]) for r in range(4)]
            for j in range(4):
                sl = slice(j * NT, (j + 1) * NT)
                m = dict(c_common)
                m["xT"] = xTs[b * 4 + j]
                m["po_fo"] = np.ascontiguousarray(np.stack([a[:, sl] for a in pf]))
                m["po_fs"] = np.ascontiguousarray(np.stack([a[:, sl] for a in ps_]))
                m["po_ro"] = np.ascontiguousarray(np.stack([a[:, sl] for a in pr]))
                m["out_aT"] = np.asarray(rA[b * 4 + j]["out_aT"])
                m["rgT"] = np.asarray(rA[b * 4 + j]["rgT"])
                m["pT"] = np.ascontiguousarray(p[i, b, sl].T)
                if j > 0:
                    hs = slice(j * NT - 2, j * NT)
                    m["h_x"] = np.ascontiguousarray(x_cur[b, hs].T)
                    m["h_fo"] = np.ascontiguousarray(np.stack([a[:, hs] for a in pf]))
                    m["h_fs"] = np.ascontiguousarray(np.stack([a[:, hs] for a in ps_]))
                    m["h_ro"] = np.ascontiguousarray(np.stack([a[:, hs] for a in pr]))
                    m["h_oa"] = np.ascontiguousarray(np.asarray(rA[b * 4 + j - 1]["out_aT"])[:, -2:])
                    m["h_rg"] = np.ascontiguousarray(np.asarray(rA[b * 4 + j - 1]["rgT"])[:, -2:])
                else:
                    m["h_x"] = np.zeros((D, 2), np.float32)
                    m["h_fo"] = np.zeros((4, 384, 2), np.float32)
                    m["h_fs"] = np.ones((4, 6, 2), np.float32)
                    m["h_ro"] = np.zeros((4, 384, 2), np.float32)
                    m["h_oa"] = np.zeros((256, 2), ml_dtypes.bfloat16)
                    m["h_rg"] = np.zeros((384, 2), ml_dtypes.bfloat16)
                maps.append(m)
        rC = _run(_prog("C", NT), maps)
        for b in range(B):
            for j in range(4):
                x_cur[b, j * NT:(j + 1) * NT] = np.asarray(rC[b * 4 + j]["outT"]).T
    return x_cur
```

```python
import math
from contextlib import ExitStack
import numpy as np
import ml_dtypes
import concourse.bass as bass
import concourse.mybir as mybir
from concourse.bass_utils import run_bass_kernel_spmd

F32 = mybir.dt.float32
BF16 = mybir.dt.bfloat16
I32 = mybir.dt.int32
AF = mybir.ActivationFunctionType
ALU = mybir.AluOpType
AX = mybir.AxisListType

D = 1024
KC = D // 128
DFF = 4096
FC = DFF // 128
DPLE = 256
EPS = 1e-6
NCORES = 8
TT = 512


class Buf:
    __slots__ = ("name", "last_write", "readers")

    def __init__(self, name=""):
        self.name = name
        self.last_write = None
        self.readers = []


class Op:
    __slots__ = ("eng", "fn", "deps", "is_dma", "need_sig", "sem", "val", "slot")

    def __init__(self, eng, fn, is_dma):
        self.eng = eng
        self.fn = fn
        self.deps = []
        self.is_dma = is_dma
        self.need_sig = is_dma
        self.sem = None
        self.val = None
        self.slot = None


class Sched:
    ENGS = ("pe", "act", "dve", "pool", "sp")
    NSLOT = 8

    def __init__(self, nc, es):
        self.nc = nc
        self.es = es
        self.ops = {e: [] for e in self.ENGS}
        self.h = {"pe": nc.tensor, "act": nc.scalar, "dve": nc.vector, "pool": nc.gpsimd, "sp": nc.sync}
        self.nbuf = 0

    def buf(self, name=""):
        return Buf(name)

    def op(self, eng, fn, reads=(), writes=(), dma=False):
        o = Op(eng, fn, dma)
        deps = set()
        for b in reads:
            if b.last_write is not None:
                deps.add(b.last_write)
        for b in writes:
            if b.last_write is not None:
                deps.add(b.last_write)
            for r in b.readers:
                deps.add(r)
        deps.discard(o)
        for d in deps:
            if d.eng == "pe" and eng == "pe" and not d.is_dma and not dma:
                continue
            o.deps.append(d)
            d.need_sig = True
        for b in reads:
            b.readers.append(o)
        for b in writes:
            b.last_write = o
            b.readers = []
        self.ops[eng].append(o)
        return o

    def emit(self):
        nc, es = self.nc, self.es
        sems = {e: es.enter_context(nc.semaphore("s_" + e)) for e in self.ENGS}
        dsems = {e: [es.enter_context(nc.semaphore("d_%s%d" % (e, i))) for i in range(self.NSLOT)]
                 for e in ("sp", "pool", "act")}
        slot_prev = {}
        for e in self.ENGS:
            cnt = 0
            dcnt = [0] * self.NSLOT
            nd = 0
            for o in self.ops[e]:
                if o.is_dma:
                    s = nd % self.NSLOT
                    nd += 1
                    dcnt[s] += 1
                    o.sem = dsems[e][s]
                    o.val = 16 * dcnt[s]
                    o.slot = (o.sem, 16 * (dcnt[s] - 1))
                elif o.need_sig:
                    cnt += 1
                    o.sem = sems[e]
                    o.val = cnt
        block = es.enter_context(nc.Block())

        def run(e):
            def body(eng):
                waited = {}
                for o in self.ops[e]:
                    ws = [(d.sem, d.val) for d in o.deps]
                    if o.is_dma and o.slot[1] > 0:
                        ws.append(o.slot)
                    best = {}
                    for s, v in ws:
                        k = id(s)
                        if v > waited.get(k, 0) and v > best.get(k, (None, 0))[1]:
                            best[k] = (s, v)
                    for k, (s, v) in best.items():
                        eng.wait_ge(s, v)
                        waited[k] = v
                    ins = o.fn(eng)
                    if o.need_sig:
                        ins.then_inc(o.sem, 16 if o.is_dma else 1)
            return body

        block.tensor(run("pe"))
        block.scalar(run("act"))
        block.vector(run("dve"))
        block.gpsimd(run("pool"))
        block.sync(run("sp"))


class Ctx:
    def __init__(self, name):
        self.nc = bass.Bass("TRN2", target_bir_lowering=False)
        self.es = ExitStack()
        self.s = Sched(self.nc, self.es)
        self.n = 0
        self.final_dmas = []

    def sb(self, shape, dt, name=None):
        self.n += 1
        t = self.es.enter_context(self.nc.sbuf_tensor(name or ("t%d" % self.n), list(shape), dt))
        return t

    def ps(self, shape=(128, 512), dt=F32, name=None):
        self.n += 1
        return self.es.enter_context(self.nc.psum_tensor(name or ("p%d" % self.n), list(shape), dt))

    def din(self, name, shape, dt):
        return self.nc.dram_tensor(name, list(shape), dt, kind="ExternalInput").ap()

    def dout(self, name, shape, dt):
        return self.nc.dram_tensor(name, list(shape), dt, kind="ExternalOutput").ap()

    def dscratch(self, name, shape, dt):
        return self.nc.dram_tensor(name, list(shape), dt, kind="Internal").ap()

    def dma(self, q, out, in_, reads=(), writes=(), final=False):
        o = self.s.op(q, lambda e: e.dma_start(out=out, in_=in_), reads, writes, dma=True)
        if final:
            self.final_dmas.append(o)
        return o

    def mm(self, out, lhsT, rhs, start, stop, reads=(), writes=()):
        return self.s.op("pe", lambda e: e.matmul(out, lhsT, rhs, start=start, stop=stop), reads, writes)

    def act(self, out, in_, func, reads=(), writes=(), bias=None, scale=None, eng="act"):
        kw = {}
        if bias is not None:
            kw["bias"] = bias
        if scale is not None:
            kw["scale"] = scale
        return self.s.op("act", lambda e: e.activation(out=out, in_=in_, func=func, **kw), reads, writes)

    def tt(self, eng, out, in0, in1, op, reads=(), writes=()):
        return self.s.op(eng, lambda e: e.tensor_tensor(out=out, in0=in0, in1=in1, op=op), reads, writes)

    def ts(self, eng, out, in0, s1, s2, op0, op1=None, reads=(), writes=()):
        if op1 is None:
            return self.s.op(eng, lambda e: e.tensor_scalar(out=out, in0=in0, scalar1=s1, scalar2=None, op0=op0),
                             reads, writes)
        return self.s.op(eng, lambda e: e.tensor_scalar(out=out, in0=in0, scalar1=s1, scalar2=s2, op0=op0, op1=op1),
                         reads, writes)

    def stt(self, eng, out, in0, scalar, in1, op0, op1, reads=(), writes=()):
        return self.s.op(eng, lambda e: e.scalar_tensor_tensor(out=out, in0=in0, scalar=scalar, in1=in1,
                                                               op0=op0, op1=op1), reads, writes)

    def copy(self, eng, out, in_, reads=(), writes=()):
        if eng == "act":
            return self.s.op("act", lambda e: e.activation(out=out, in_=in_, func=AF.Copy), reads, writes)
        return self.s.op(eng, lambda e: e.tensor_copy(out=out, in_=in_), reads, writes)

    def recip(self, out, in_, reads=(), writes=()):
        return self.s.op("dve", lambda e: e.reciprocal(out=out, in_=in_), reads, writes)

    def memset(self, eng, ap, val, writes=()):
        return self.s.op(eng, lambda e: e.memset(ap, val), (), writes)

    def finish(self):
        bdone = [Buf("done")]
        if self.final_dmas:
            fd = list(self.final_dmas)
            o = self.s.op("sp", lambda e: e.nop(), (), ())
            for d in fd:
                o.deps.append(d)
        self.s.emit()
        self.es.close()
        return self.nc


class Common:
    def __init__(self, c: Ctx):
        self.c = c
        self.ones = c.sb([128, 128], BF16, "ones_bf")
        self.b_ones = Buf("ones")
        c.memset("pool", self.ones[:], 1.0, [self.b_ones])
        self.epsb = c.sb([128, 1], F32, "eps_b")
        self.b_eps = Buf("eps")
        c.memset("pool", self.epsb[:], EPS, [self.b_eps])
        self.sq = c.sb([128, KC, TT], BF16, "nsq")
        self.b_sq = Buf("sq")
        self.ssp = c.ps(name="ss_ps")
        self.b_ssp = Buf("ssp")
        self.rt = c.sb([128, TT], F32, "nrt")
        self.b_rt = Buf("rt")

    def rstd(self, src, b_src, out, b_out, nchunk=KC, width=TT, dim=D, rows=128):
        c = self.c
        c.act(self.sq[0:rows, 0:nchunk, 0:width], src, AF.Square, [b_src], [self.b_sq])
        for k in range(nchunk):
            c.mm(self.ssp[0:rows, 0:width], self.ones[0:rows, 0:rows], self.sq[0:rows, k, 0:width],
                 k == 0, k == nchunk - 1, [self.b_sq, self.b_ones], [self.b_ssp])
        c.act(self.rt[0:rows, 0:width], self.ssp[0:rows, 0:width], AF.Sqrt, [self.b_ssp, self.b_eps], [self.b_rt],
              bias=self.epsb[0:rows, :], scale=1.0 / dim)
        c.recip(out, self.rt[0:rows, 0:width], [self.b_rt], [b_out])


def load_vec_fm(c, q, dram_ap, nchunk, name):
    t = c.sb([128, nchunk], F32, name)
    b = Buf(name)
    c.dma(q, t[:], dram_ap, (), [b])
    return t, b


def build_C(NT):
    c = Ctx("C")
    NP = 1
    PT = NT
    ntile = NT // TT
    xT_in = c.din("xT", [D, NT], F32)
    po_fo = c.din("po_fo", [4, 384, NT], F32)
    po_fs = c.din("po_fs", [4, 6, NT], F32)
    po_ro = c.din("po_ro", [4, 384, NT], F32)
    oaT = c.din("out_aT", [256, NT], BF16)
    rgT = c.din("rgT", [384, NT], BF16)
    h_x = c.din("h_x", [D, 2], F32)
    h_fo = c.din("h_fo", [4, 384, 2], F32)
    h_fs = c.din("h_fs", [4, 6, 2], F32)
    h_ro = c.din("h_ro", [4, 384, 2], F32)
    h_oa = c.din("h_oa", [256, 2], BF16)
    h_rg = c.din("h_rg", [384, 2], BF16)
    w_o = c.din("w_o", [KC, 128, KC, 128], F32)
    g_mpost = c.din("mix_post_g", [128, KC], F32)
    selb_d = c.din("selb", [6, 3, 128], F32)
    blk_d = c.din("blk", [128, 128], F32)
    pT = c.din("pT", [DPLE, NT], F32)
    wg = c.din("w_gate", [FC, 128, KC, 128], F32)
    wu = c.din("w_up", [FC, 128, KC, 128], F32)
    wd = c.din("w_down", [KC, 128, FC, 128], F32)
    wpg = c.din("w_ple_gate", [KC, 128, KC, 128], F32)
    wpp = c.din("w_ple_proj", [KC, 128, 2, 128], F32)
    convw = c.din("conv_w", [128, FC, 3], F32)
    convb = c.din("conv_b", [128, FC], F32)
    g_pre = c.din("ffn_pre_g", [128, KC], F32)
    g_post = c.din("ffn_post_g", [128, KC], F32)
    g_ppre = c.din("ple_pre_g", [128, KC], F32)
    g_ppost = c.din("ple_post_g", [128, KC], F32)
    outT = c.dout("outT", [D, NT], F32)
    wg_s = c.dscratch("wg_s", [FC, 128, KC * 128], BF16)
    wu_s = c.dscratch("wu_s", [FC, 128, KC * 128], BF16)
    wd_s = c.dscratch("wd_s", [KC, 128, FC * 128], BF16)

    cm = Common(c)
    gpre, b_gpre = load_vec_fm(c, "sp", g_pre, KC, "gpre")
    gpost, b_gpost = load_vec_fm(c, "sp", g_post, KC, "gpost")
    gppre, b_gppre = load_vec_fm(c, "sp", g_ppre, KC, "gppre")
    gppost, b_gppost = load_vec_fm(c, "sp", g_ppost, KC, "gppost")
    cw = c.sb([128, FC, 3], F32, "cw")
    b_cw = Buf()
    c.dma("sp", cw[:], convw, (), [b_cw])
    cb = c.sb([128, FC], F32, "cb")
    b_cb = Buf()
    c.dma("sp", cb[:], convb, (), [b_cb])
    gmpost, b_gmpost = load_vec_fm(c, "sp", g_mpost, KC, "gmpost")
    wo_t = c.sb([128, KC, KC * 128], BF16, "wo")
    b_wo = Buf()
    for o in range(KC):
        c.dma("pool", wo_t[:, o, :], w_o[o].rearrange("p k c -> p (k c)"), (), [b_wo])
    selb = c.sb([6, 3, 128], F32, "selb_sb")
    b_selb = Buf()
    c.dma("sp", selb[:], selb_d, (), [b_selb])
    blk = c.sb([128, 128], BF16, "blk_sb")
    b_blk = Buf()
    c.dma("pool", blk[:], blk_d, (), [b_blk])
    wpg_t = c.sb([128, KC, KC * 128], BF16, "wpg")
    b_wpg = Buf()
    for o in range(KC):
        c.dma("pool", wpg_t[:, o, :], wpg[o].rearrange("p k c -> p (k c)"), (), [b_wpg])
    wpp_t = c.sb([128, KC, 2 * 128], BF16, "wpp")
    b_wpp = Buf()
    for o in range(KC):
        c.dma("pool", wpp_t[:, o, :], wpp[o].rearrange("p k c -> p (k c)"), (), [b_wpp])
    NCV = 2
    wdt = [c.sb([128, FC * 128], BF16, "wdt%d" % i) for i in range(2)]
    b_wdt = [Buf() for _ in range(2)]
    cv, b_cv = wdt, b_wdt
    b_wgs = [Buf() for _ in range(FC)]
    b_wus = [Buf() for _ in range(FC)]
    b_wds = [Buf() for _ in range(KC)]
    ci = 0
    for (src, dst, bl, n, w) in ((wg, wg_s, b_wgs, FC, KC * 128), (wu, wu_s, b_wus, FC, KC * 128),
                                 (wd, wd_s, b_wds, KC, FC * 128)):
        step = (FC * 128) // w
        for i0 in range(0, n, step):
            s = ci % NCV
            ci += 1
            for j in range(step):
                c.dma("pool", cv[s][:, j * w:(j + 1) * w], src[i0 + j].rearrange("p k c -> p (k c)"), (), [b_cv[s]])
            for j in range(step):
                c.dma("sp", dst[i0 + j], cv[s][:, j * w:(j + 1) * w], [b_cv[s]], [bl[i0 + j]])

    xt = [c.sb([128, KC, TT], F32, "xt0")] * 2
    b_xt = [Buf()] * 2
    hT = c.sb([128, KC, TT], BF16, "hT")
    b_hT = Buf()
    rs = c.sb([128, TT], F32, "rs")
    b_rs = Buf()
    actT = c.sb([128, FC, TT], BF16, "actT")
    b_act = [Buf() for _ in range(FC)]
    fT = c.sb([128, KC, TT], F32, "fT")
    b_fT = Buf()
    tmp = c.sb([128, TT], F32, "tmp")
    b_tmp = Buf()
    gs = [c.sb([128, TT + 2], F32, "gs%d" % i) for i in range(2)]
    b_gs = [Buf() for _ in range(2)]
    c1 = [c.sb([128, TT], F32, "c1_%d" % i) for i in range(2)]
    b_c1 = [Buf() for _ in range(2)]
    c2 = [c.sb([128, TT], F32, "c2_0")] * 2
    b_c2 = [Buf()] * 2
    gcar = c.sb([128, FC, 2], F32, "gcar")
    b_gcar = [Buf() for _ in range(FC)]
    pt = c.sb([128, 2, TT], BF16, "pt")
    b_pt = Buf()
    hx = c.sb([128, KC, 2], F32, "hx")
    b_hx = Buf()
    hh = c.sb([128, KC, 2], BF16, "hh")
    b_hh = Buf()
    hrs = c.sb([128, 2], F32, "hrs")
    b_hrs = Buf()
    NW = 2
    wgt = [c.sb([128, KC * 128], BF16, "wgt%d" % i) for i in range(NW)]
    b_wgt = [Buf() for _ in range(NW)]
    wut = [c.sb([128, KC * 128], BF16, "wut%d" % i) for i in range(NW)]
    b_wut = [Buf() for _ in range(NW)]
    gps = [c.ps(name="gps%d" % i) for i in range(2)]
    b_gps = [Buf() for _ in range(2)]
    ups = [c.ps(name="ups%d" % i) for i in range(2)]
    b_ups = [Buf() for _ in range(2)]
    ops_ = [c.ps(name="ops%d" % i) for i in range(2)]
    b_ops = [Buf() for _ in range(2)]

    mixT = c.sb([128, KC, TT], BF16, "mixT")
    b_mix = Buf()
    accf = c.sb([128, 3, TT], F32, "accf")
    b_accf = Buf()
    accr = c.sb([128, 3, TT], F32, "accr")
    b_accr = Buf()
    stg = [c.sb([128, 3, TT], F32, "stg0")] * 2
    b_stg = [Buf()] * 2
    fs4 = c.sb([6, 4, TT], F32, "fs4")
    b_fs4 = Buf()
    rs6 = c.sb([6, TT], F32, "rs6")
    b_rs6 = Buf()
    rgt = c.sb([128, 3, TT], BF16, "rgt")
    b_rgt = Buf()
    sq3 = c.sb([128, 3, TT], BF16, "sq3")
    b_sq3 = Buf()
    rr = c.sb([128, TT], F32, "rr")
    b_rr = Buf()
    stc = [0]

    def combine(X, bX, w, x_src, fo_src, fs_src, ro_src, oa_src, rg_src):
        c.dma("sp", X[:, :, 0:w], x_src.rearrange("(k p) t -> p k t", p=128), (), [bX])
        c.dma("sp", mixT[:, 0:2, 0:w], oa_src.rearrange("(k p) t -> p k t", p=128), (), [b_mix])
        c.dma("sp", rgt[:, :, 0:w], rg_src.rearrange("(k p) t -> p k t", p=128), (), [b_rgt])
        c.dma("sp", fs4[:, :, 0:w], fs_src.rearrange("r h t -> h r t"), (), [b_fs4])
        for (src, acc, b_acc) in ((fo_src, accf, b_accf), (ro_src, accr, b_accr)):
            c.dma("sp", acc[:, :, 0:w], src[0].rearrange("(k p) t -> p k t", p=128), (), [b_acc])
            for r in range(1, 4):
                i = stc[0] % 2
                stc[0] += 1
                c.dma("sp", stg[i][:, :, 0:w], src[r].rearrange("(k p) t -> p k t", p=128), (), [b_stg[i]])
                c.tt("pool" if r == 2 else "dve", acc[:, :, 0:w], acc[:, :, 0:w], stg[i][:, :, 0:w], ALU.add,
                     [b_acc, b_stg[i]], [b_acc])
        c.tt("dve", rs6[:, 0:w], fs4[:, 0, 0:w], fs4[:, 1, 0:w], ALU.add, [b_fs4], [b_rs6])
        c.tt("dve", rs6[:, 0:w], rs6[:, 0:w], fs4[:, 2, 0:w], ALU.add, [b_fs4, b_rs6], [b_rs6])
        c.tt("dve", rs6[:, 0:w], rs6[:, 0:w], fs4[:, 3, 0:w], ALU.add, [b_fs4, b_rs6], [b_rs6])
        c.recip(rs6[:, 0:w], rs6[:, 0:w], [b_rs6], [b_rs6])
        for j in range(3):
            i = j % 2
            c.mm(gps[i][:, 0:w], selb[:, j, :], rs6[:, 0:w], True, True, [b_selb, b_rs6], [b_gps[i]])
            c.tt("dve", mixT[:, 5 + j, 0:w], accf[:, j, 0:w], gps[i][:, 0:w], ALU.mult, [b_accf, b_gps[i]], [b_mix])
        c.act(sq3[:, :, 0:w], accr[:, :, 0:w], AF.Square, [b_accr], [b_sq3])
        for j in range(3):
            i = j % 2
            c.mm(ups[i][:, 0:w], blk[:], sq3[:, j, 0:w], True, True, [b_blk, b_sq3], [b_ups[i]])
            c.act(rr[:, 0:w], ups[i][:, 0:w], AF.Sqrt, [b_ups[i], cm.b_eps], [b_rr], bias=cm.epsb[:], scale=1.0 / 64)
            c.recip(rr[:, 0:w], rr[:, 0:w], [b_rr], [b_rr])
            c.tt("dve", rr[:, 0:w], rr[:, 0:w], accr[:, j, 0:w], ALU.mult, [b_rr, b_accr], [b_rr])
            c.tt("pool", mixT[:, 2 + j, 0:w], rr[:, 0:w], rgt[:, j, 0:w], ALU.mult, [b_rr, b_rgt], [b_mix])
        for o in range(KC):
            P, bP = ops_[o % 2], b_ops[o % 2]
            for k in range(KC):
                c.mm(P[:, 0:w], wo_t[:, o, k * 128:(k + 1) * 128], mixT[:, k, 0:w], k == 0, k == KC - 1,
                     [b_wo, b_mix], [bP])
            c.copy("act", fT[:, o, 0:w], P[:, 0:w], [bP], [b_fT])
        cm.rstd(fT[:, :, 0:w], b_fT, rs[:, 0:w], b_rs, width=w)
        for k in range(KC):
            c.stt("dve", tmp[:, 0:w], fT[:, k, 0:w], gmpost[:, k:k + 1], rs[:, 0:w], ALU.mult, ALU.mult,
                  [b_fT, b_gmpost, b_rs], [b_tmp])
            c.tt("pool", X[:, k, 0:w], X[:, k, 0:w], tmp[:, 0:w], ALU.add, [bX, b_tmp], [bX])

    def make_h(src, b_src, gain, b_gain, dst, b_dst, rs_, b_rs_, width):
        for k in range(KC):
            c.stt("dve", dst[:, k, 0:width], src[:, k, 0:width], gain[:, k:k + 1], rs_[:, 0:width],
                  ALU.mult, ALU.mult, [b_src, b_gain, b_rs_], [b_dst])

    wcount = [0]
    for t in range(ntile):
        X, bX = xt[t % 2], b_xt[t % 2]
        first_in_piece = (t * TT) % PT == 0
        piece = (t * TT) // PT
        cols = slice(t * TT, (t + 1) * TT)
        c.dma("pool", pt[:], pT[:, cols].rearrange("(k p) t -> p k t", p=128), (), [b_pt])
        if first_in_piece:
            combine(hx, b_hx, 2, h_x, h_fo, h_fs, h_ro, h_oa, h_rg)
            cm.rstd(hx[:], b_hx, hrs[:], b_hrs, width=2)
            make_h(hx, b_hx, gpre, b_gpre, hh, b_hh, hrs, b_hrs, 2)
        combine(X, bX, TT, xT_in[:, cols], po_fo[:, :, cols], po_fs[:, :, cols], po_ro[:, :, cols],
                oaT[:, cols], rgT[:, cols])
        cm.rstd(X[:], bX, rs[:], b_rs)
        make_h(X, bX, gpre, b_gpre, hT, b_hT, rs, b_rs, TT)
        def ffn_mm(f):
            w = wcount[0] % NW
            wcount[0] += 1
            c.dma("sp", wgt[w][:], wg_s[f], [b_wgs[f]], [b_wgt[w]])
            c.dma("sp", wut[w][:], wu_s[f], [b_wus[f]], [b_wut[w]])
            i = f % 2
            if first_in_piece:
                for k in range(KC):
                    c.mm(gps[i][:, 0:2], wgt[w][:, k * 128:(k + 1) * 128], hh[:, k, :], k == 0, k == KC - 1,
                         [b_wgt[w], b_hh], [b_gps[i]])
                c.copy("act", gcar[:, f, :], gps[i][:, 0:2], [b_gps[i]], [b_gcar[f]])
            for k in range(KC):
                c.mm(gps[i][:], wgt[w][:, k * 128:(k + 1) * 128], hT[:, k, :], k == 0, k == KC - 1,
                     [b_wgt[w], b_hT], [b_gps[i]])
            for k in range(KC):
                c.mm(ups[i][:], wut[w][:, k * 128:(k + 1) * 128], hT[:, k, :], k == 0, k == KC - 1,
                     [b_wut[w], b_hT], [b_ups[i]])

        def ffn_ep1(f):
            i = f % 2
            G, bG = gs[i], b_gs[i]
            c.copy("pool", G[:, 0:2], gcar[:, f, :], [b_gcar[f]], [bG])
            c.copy("act", G[:, 2:TT + 2], gps[i][:], [b_gps[i]], [bG])
            c.copy("pool", gcar[:, f, :], G[:, TT:TT + 2], [bG], [b_gcar[f]])
            c.act(c1[i][:], G[:, 2:TT + 2], AF.Identity, [bG, b_cw, b_cb], [b_c1[i]],
                  bias=cb[:, f:f + 1], scale=cw[:, f, 2:3])
            c.stt("dve", c1[i][:], G[:, 1:TT + 1], cw[:, f, 1:2], c1[i][:], ALU.mult, ALU.add,
                  [bG, b_cw, b_c1[i]], [b_c1[i]])
            c.stt("dve", c1[i][:], G[:, 0:TT], cw[:, f, 0:1], c1[i][:], ALU.mult, ALU.add,
                  [bG, b_cw, b_c1[i]], [b_c1[i]])

        def ffn_ep2(f):
            i = f % 2
            c.act(c2[i][:], c1[i][:], AF.Gelu_apprx_tanh, [b_c1[i]], [b_c2[i]])
            c.tt("dve", actT[:, f, :], c2[i][:], ups[i][:], ALU.mult, [b_c2[i], b_ups[i]], [b_act[f]])

        for f in range(FC + 1):
            if f < FC:
                ffn_mm(f)
            if f >= 1:
                ffn_ep2(f - 1)
            if f < FC:
                ffn_ep1(f)
        for o in range(KC):
            w = o % 2
            c.dma("sp", wdt[w][:], wd_s[o], [b_wds[o]], [b_wdt[w]])
            P, bP = ops_[o % 2], b_ops[o % 2]
            for f in range(FC):
                c.mm(P[:], wdt[w][:, f * 128:(f + 1) * 128], actT[:, f, :], f == 0, f == FC - 1,
                     [b_wdt[w], b_act[f]], [bP])
            c.copy("act", fT[:, o, :], P[:], [bP], [b_fT])
        cm.rstd(fT[:], b_fT, rs[:], b_rs)
        for k in range(KC):
            c.stt("dve", tmp[:], fT[:, k, :], gpost[:, k:k + 1], rs[:], ALU.mult, ALU.mult,
                  [b_fT, b_gpost, b_rs], [b_tmp])
            c.tt("pool", X[:, k, :], X[:, k, :], tmp[:], ALU.add, [bX, b_tmp], [bX])
        cm.rstd(X[:], bX, rs[:], b_rs)
        make_h(X, bX, gppre, b_gppre, hT, b_hT, rs, b_rs, TT)
        for o in range(KC):
            i = o % 2
            for k in range(KC):
                c.mm(gps[i][:], wpg_t[:, o, k * 128:(k + 1) * 128], hT[:, k, :], k == 0, k == KC - 1,
                     [b_wpg, b_hT], [b_gps[i]])
            for k in range(2):
                c.mm(ups[i][:], wpp_t[:, o, k * 128:(k + 1) * 128], pt[:, k, :], k == 0, k == 1,
                     [b_wpp, b_pt], [b_ups[i]])
            c.act(c1[i][:], gps[i][:], AF.Sigmoid, [b_gps[i]], [b_c1[i]])
            c.tt("dve", fT[:, o, :], c1[i][:], ups[i][:], ALU.mult, [b_c1[i], b_ups[i]], [b_fT])
        cm.rstd(fT[:], b_fT, rs[:], b_rs)
        for k in range(KC):
            c.stt("dve", tmp[:], fT[:, k, :], gppost[:, k:k + 1], rs[:], ALU.mult, ALU.mult,
                  [b_fT, b_gppost, b_rs], [b_tmp])
            c.tt("pool", X[:, k, :], X[:, k, :], tmp[:], ALU.add, [bX, b_tmp], [bX])
        c.dma("sp", outT[:, t * TT:(t + 1) * TT].rearrange("(k p) t -> p k t", p=128), X[:], [bX], (), final=True)
    return c.finish()


NIN = 3206
C_AU, C_AV, C_RQ, C_RK, C_RV, C_RG, C_FQ, C_FK, C_FV, C_FF = 0, 256, 512, 896, 1280, 1664, 2048, 2432, 2816, 3200
C_RH = 3208
NWC = C_RH + 768
TWO_PI = 2.0 * math.pi


def build_A(NT, NP):
    c = Ctx("A")
    PT = NT // NP
    ntile = NT // TT
    xT = c.din("xT", [D, NT], F32)
    posb = c.din("posb", [128, NT], I32)
    w_in = c.din("w_in", [D, NIN], F32)
    g_pre = c.din("mix_pre_g", [128, KC], F32)
    vgb = c.din("sgu_vg_b", [128, 256], F32)
    swT = c.din("sgu_wT", [4, 128, 128], F32)
    sbias = c.din("sgu_b", [2, 2, 128], F32)
    fbf = c.din("fox_bf", [6, 1], F32)
    invf_d = c.din("invf", [128, 1], F32)
    gq_d = c.din("gq", [3, 128, TT], F32)
    gk_d = c.din("gk", [3, 128, TT], F32)
    tri_d = c.din("tri", [128, 128], F32)
    e2_d = c.din("e2", [2, 128], F32)
    o_aT = c.dout("out_aT", [256, NT], BF16)
    o_rq = c.dout("rqT", [384, NT], BF16)
    o_rk = c.dout("rkT", [384, NT], BF16)
    o_rg = c.dout("rgT", [384, NT], BF16)
    o_fq = c.dout("fqT", [384, NT], BF16)
    o_fk = c.dout("fkT", [384, NT], BF16)
    o_v = c.dout("vtok", [NT, 768], BF16)
    o_cl = c.dout("cl", [6, NT], F32)

    cm = Common(c)
    gpre, b_gpre = load_vec_fm(c, "sp", g_pre, KC, "gpre")
    W = c.sb([128, KC, NWC], BF16, "W")
    b_W = Buf()
    for k in range(KC):
        c.dma("pool", W[:, k, 0:NIN], w_in[k * 128:(k + 1) * 128, :], (), [b_W])
    for k in range(KC):
        src = W[:, k, C_RQ:C_RQ + 768].rearrange("p (h d) -> p h d", d=64)
        dst = W[:, k, C_RH:C_RH + 768].rearrange("p (h d) -> p h d", d=64)
        c.act(dst[:, :, 0:32], src[:, :, 32:64], AF.Copy, [b_W], [b_W], scale=-1.0)
        c.copy("dve", dst[:, :, 32:64], src[:, :, 0:32], [b_W], [b_W])
    def ld(name, shape, src, q="sp", dt=F32):
        t = c.sb(shape, dt, name + "_sb")
        b = Buf()
        c.dma(q, t[:], src, (), [b])
        return t, b
    invf, b_invf = ld("invf", [128, 1], invf_d)
    gq, b_gq = ld("gq", [128, 3, TT], gq_d.rearrange("j p t -> p j t"))
    gk, b_gk = ld("gk", [128, 3, TT], gk_d.rearrange("j p t -> p j t"))
    tri, b_tri = ld("tri", [128, 128], tri_d)
    e2, b_e2 = ld("e2", [2, 128], e2_d, q="pool", dt=BF16)
    vg_b, b_vgb = ld("vgb", [128, 256], vgb)
    fb, b_fb = ld("fb", [6, 1], fbf)
    nfb = c.sb([6, 1], F32, "nfb")
    b_nfb = Buf()
    c.ts("dve", nfb[:], fb[:], -1.0, None, ALU.mult, None, [b_fb], [b_nfb])
    mpi = c.sb([128, 1], F32, "mpi")
    b_mpi = Buf()
    c.memset("pool", mpi[:], -math.pi, [b_mpi])
    one6 = c.sb([6, 1], F32, "one6")
    b_one6 = Buf()
    c.memset("pool", one6[:], 1.0, [b_one6])
    ones_row = c.sb([6, TT], F32, "ones_row")
    b_onesr = Buf()
    c.memset("pool", ones_row[:], 1.0, [b_onesr])
    swf = c.sb([128, 4, 128], F32, "swf")
    b_swf = Buf()
    c.dma("sp", swf[:], swT.rearrange("h s t -> s h t"), (), [b_swf])
    swm = c.sb([128, 4, 128], BF16, "swm")
    b_swm = Buf()
    for h in range(4):
        c.tt("dve", swm[:, h, :], swf[:, h, :], tri[:], ALU.mult, [b_swf, b_tri], [b_swm])
    bf_ = c.sb([2, 2, 128], F32, "sbf")
    b_bf = Buf()
    c.dma("sp", bf_[:], sbias.rearrange("j h t -> h j t"), (), [b_bf])
    bhi = c.sb([2, 2, 128], BF16, "bhi")
    bback = c.sb([2, 2, 128], F32, "bback")
    blo = c.sb([2, 2, 128], BF16, "blo")
    b_bhi, b_bback, b_blo = Buf(), Buf(), Buf()
    c.copy("dve", bhi[:], bf_[:], [b_bf], [b_bhi])
    c.copy("dve", bback[:], bhi[:], [b_bhi], [b_bback])
    c.tt("dve", blo[:], bf_[:], bback[:], ALU.subtract, [b_bf, b_bback], [b_blo])

    xt = [c.sb([128, KC, TT], F32, "xt%d" % i) for i in range(2)]
    b_xt = [Buf() for _ in range(2)]
    hT = c.sb([128, KC, TT], BF16, "hT")
    b_hT = Buf()
    rs = c.sb([128, TT], F32, "rs")
    b_rs = Buf()
    posi = c.sb([128, TT], I32, "posi")
    b_posi = Buf()
    posf = c.sb([128, TT], F32, "posf")
    b_posf = Buf()
    ang = c.sb([128, TT], F32, "ang")
    b_ang = Buf()
    kint = c.sb([128, TT], I32, "kint")
    b_kint = Buf()
    ang2 = c.sb([128, TT], F32, "ang2")
    b_ang2 = Buf()
    sinT = c.sb([128, TT], F32, "sinT")
    b_sin = Buf()
    cosT = c.sb([128, TT], F32, "cosT")
    b_cos = Buf()
    NF = 3
    fps = [c.ps(name="fps%d" % i) for i in range(NF)]
    b_fps = [Buf() for _ in range(NF)]
    vps = [c.ps(name="vps%d" % i) for i in range(2)]
    b_vps = [Buf() for _ in range(2)]
    sps = [c.ps(name="sps%d" % i) for i in range(2)]
    b_sps = [Buf() for _ in range(2)]
    NOB = 4
    ob = [c.sb([128, TT], BF16, "ob%d" % i) for i in range(NOB)]
    b_ob = [Buf() for _ in range(NOB)]
    guT = [c.sb([128, TT], F32, "guT%d" % i) for i in range(2)]
    b_gu = [Buf() for _ in range(2)]
    vgt = c.sb([128, 256], F32, "vgt")
    b_vgt = Buf()
    sqv = c.sb([128, 256], F32, "sqv")
    b_sqv = Buf()
    ssv = c.sb([128, 4], F32, "ssv")
    b_ssv = Buf()
    rtv = c.sb([128, 4], F32, "rtv")
    b_rtv = Buf()
    vpad = [c.sb([128, 4, 128], BF16, "vpad%d" % i) for i in range(2)]
    b_vpad = [Buf() for _ in range(2)]
    for i in range(2):
        c.memset("pool", vpad[i][:], 0.0, [b_vpad[i]])
    vout = [c.sb([128, 768], BF16, "vout%d" % i) for i in range(2)]
    b_vout = [Buf() for _ in range(2)]
    t1 = c.sb([128, TT], F32, "t1")
    b_t1 = Buf()
    t2 = c.sb([128, TT], F32, "t2")
    b_t2 = Buf()
    t3 = c.sb([128, TT], F32, "t3")
    b_t3 = Buf()
    fe = c.sb([6, TT], F32, "fe")
    b_fe = Buf()
    fcs = [c.sb([6, TT], F32, "fcs%d" % i) for i in range(2)]
    b_fcs = [Buf() for _ in range(2)]
    fneg = c.sb([6, TT], F32, "fneg")
    b_fneg = Buf()

    cnt = {"f": 0, "o": 0, "sb": 0}

    def next_f():
        i = cnt["f"] % NF
        cnt["f"] += 1
        return fps[i], b_fps[i]

    def next_o():
        i = cnt["o"] % NOB
        cnt["o"] += 1
        return ob[i], b_ob[i]

    def fm_mm(P, bP, col0, ncol=128):
        for k in range(KC):
            c.mm(P[0:ncol, :], W[:, k, col0:col0 + ncol], hT[:, k, :], k == 0, k == KC - 1, [b_W, b_hT], [bP])

    for t in range(ntile):
        X, bX = xt[t % 2], b_xt[t % 2]
        cols = slice(t * TT, (t + 1) * TT)
        first_in_piece = (t * TT) % PT == 0
        c.dma("sp", X[:], xT[:, cols].rearrange("(k p) t -> p k t", p=128), (), [bX])
        c.dma("sp", posi[:], posb[:, cols], (), [b_posi])
        cm.rstd(X[:], bX, rs[:], b_rs)
        for k in range(KC):
            c.stt("dve", hT[:, k, :], X[:, k, :], gpre[:, k:k + 1], rs[:], ALU.mult, ALU.mult,
                  [bX, b_gpre, b_rs], [b_hT])
        c.copy("pool", posf[:], posi[:], [b_posi], [b_posf])
        C1 = 6.28125
        C2 = TWO_PI - C1
        for (shift, dstT, b_dst) in ((0.0, sinT, b_sin), (0.5 * math.pi, cosT, b_cos)):
            c.ts("dve", ang[:], posf[:], invf[:, 0:1], shift, ALU.mult, ALU.add, [b_posf, b_invf], [b_ang])
            c.ts("dve", ang2[:], ang[:], 1.0 / TWO_PI, None, ALU.mult, None, [b_ang], [b_ang2])
            c.copy("dve", kint[:], ang2[:], [b_ang2], [b_kint])
            c.copy("dve", ang2[:], kint[:], [b_kint], [b_ang2])
            c.stt("dve", ang[:], ang2[:], -C1, ang[:], ALU.mult, ALU.add, [b_ang2, b_ang], [b_ang])
            c.stt("dve", ang[:], ang2[:], -C2, ang[:], ALU.mult, ALU.add, [b_ang2, b_ang], [b_ang])
            c.ts("dve", ang[:], ang[:], -math.pi, math.pi, ALU.max, ALU.min, [b_ang], [b_ang])
            c.act(dstT[:], ang[:], AF.Sin, [b_ang], [b_dst])
        for sbk in range(4):
            tok = slice(sbk * 128, (sbk + 1) * 128)
            g0, bg0 = vps[0], b_vps[0]
            g1, bg1 = vps[1], b_vps[1]
            for k in range(KC):
                c.mm(g0[:, 0:256], hT[:, k, tok], W[:, k, C_AV:C_AV + 256], k == 0, k == KC - 1, [b_hT, b_W], [bg0])
            for k in range(KC):
                c.mm(g0[:, 256:512], hT[:, k, tok], W[:, k, C_RV:C_RV + 256], k == 0, k == KC - 1,
                     [b_hT, b_W], [bg0])
            for k in range(KC):
                c.mm(g1[:, 0:128], hT[:, k, tok], W[:, k, C_RV + 256:C_RV + 384], k == 0, k == KC - 1,
                     [b_hT, b_W], [bg1])
            for k in range(KC):
                c.mm(g1[:, 128:512], hT[:, k, tok], W[:, k, C_FV:C_FV + 384], k == 0, k == KC - 1,
                     [b_hT, b_W], [bg1])
            i = cnt["sb"] % 2
            cnt["sb"] += 1
            VO, bVO = vout[i], b_vout[i]
            c.copy("dve", VO[:, 0:256], g0[:, 256:512], [bg0], [bVO])
            c.copy("act", VO[:, 256:768], g1[:, 0:512], [bg1], [bVO])
            c.dma("sp", o_v[t * TT + sbk * 128:t * TT + (sbk + 1) * 128, :], VO[:], [bVO], (), final=True)
            c.act(vgt[:], g0[:, 0:256], AF.Gelu_apprx_tanh, [bg0], [b_vgt])
            c.tt("pool", sqv[:], vgt[:], vgt[:], ALU.mult, [b_vgt], [b_sqv])
            c.s.op("dve", lambda e: e.tensor_reduce(out=ssv[:], in_=sqv[:].rearrange("p (h d) -> p h d", d=64),
                                                    axis=AX.X, op=ALU.add), [b_sqv], [b_ssv])
            c.act(rtv[:], ssv[:], AF.Sqrt, [b_ssv, cm.b_eps], [b_rtv], bias=cm.epsb[:], scale=1.0 / 64)
            c.recip(ssv[:], rtv[:], [b_rtv], [b_ssv])
            VP, bVP = vpad[i], b_vpad[i]
            for h in range(4):
                off = 64 * (h % 2)
                c.stt("dve", VP[:, h, off:off + 64], vgt[:, h * 64:(h + 1) * 64], ssv[:, h:h + 1],
                      vg_b[:, h * 64:(h + 1) * 64], ALU.mult, ALU.mult, [b_vgt, b_ssv, b_vgb], [bVP])
            for j in range(2):
                so = sps[j][:, tok]
                c.mm(so, VP[:, 2 * j, :], swm[:, 2 * j, :], True, False, [bVP, b_swm], [b_sps[j]])
                c.mm(so, VP[:, 2 * j + 1, :], swm[:, 2 * j + 1, :], False, False, [bVP, b_swm], [b_sps[j]])
                c.mm(so, e2[:], bhi[:, j, :], False, False, [b_e2, b_bhi], [b_sps[j]])
                c.mm(so, e2[:], blo[:, j, :], False, True, [b_e2, b_blo], [b_sps[j]])
        for j in range(2):
            P, bP = next_f()
            fm_mm(P, bP, C_AU + 128 * j)
            c.act(guT[j][:], P[:], AF.Gelu_apprx_tanh, [bP], [b_gu[j]])
            O, bO = next_o()
            c.tt("dve", O[:], guT[j][:], sps[j][:], ALU.mult, [b_gu[j], b_sps[j]], [bO])
            c.dma("sp", o_aT[j * 128:(j + 1) * 128, cols], O[:], [bO], (), final=True)
        for (c0, rh0, G, bG, dst) in ((C_RQ, C_RH, gq, b_gq, o_rq), (C_RK, C_RH + 384, gk, b_gk, o_rk)):
            for j in range(3):
                P, bP = next_f()
                fm_mm(P, bP, c0 + 128 * j)
                P2, bP2 = next_f()
                fm_mm(P2, bP2, rh0 + 128 * j)
                c.tt("dve", t1[:], P[:], cosT[:], ALU.mult, [bP, b_cos], [b_t1])
                c.copy("act", t2[:], P2[:], [bP2], [b_t2])
                c.tt("pool", t3[:], t2[:], sinT[:], ALU.mult, [b_t2, b_sin], [b_t3])
                c.tt("pool", t1[:], t1[:], t3[:], ALU.add, [b_t1, b_t3], [b_t1])
                O, bO = next_o()
                c.tt("pool", O[:], t1[:], G[:, j, :], ALU.mult, [b_t1, bG], [bO])
                c.dma("sp", dst[j * 128:(j + 1) * 128, cols], O[:], [bO], (), final=True)
        for (c0, dst, func, scale) in ((C_RG, o_rg, AF.Silu, None), (C_FQ, o_fq, AF.Copy, 0.125),
                                        (C_FK, o_fk, AF.Copy, None)):
            for j in range(3):
                P, bP = next_f()
                fm_mm(P, bP, c0 + 128 * j)
                O, bO = next_o()
                c.act(O[:], P[:], func, [bP], [bO], scale=scale)
                c.dma("sp", dst[j * 128:(j + 1) * 128, cols], O[:], [bO], (), final=True)
        P, bP = next_f()
        fm_mm(P, bP, C_FF, 6)
        c.act(fe[:], P[0:6, :], AF.Exp, [bP, b_nfb], [b_fe], bias=nfb[:], scale=-1.0)
        c.act(fe[:], fe[:], AF.Ln, [b_fe, b_one6], [b_fe], bias=one6[:])
        FC_, bFC = fcs[t % 2], b_fcs[t % 2]
        prev, bprev = fcs[(t + 1) % 2], b_fcs[(t + 1) % 2]
        if first_in_piece:
            c.s.op("dve", lambda e, FC_=FC_: e.tensor_tensor_scan(out=FC_[:], data0=ones_row[:], data1=fe[:],
                                                                  initial=0.0, op0=ALU.mult, op1=ALU.add),
                   [b_onesr, b_fe], [bFC])
        else:
            c.s.op("dve", lambda e, FC_=FC_, prev=prev: e.tensor_tensor_scan(
                out=FC_[:], data0=ones_row[:], data1=fe[:], initial=prev[:, TT - 1:TT], op0=ALU.mult, op1=ALU.add),
                [b_onesr, b_fe, bprev], [bFC])
        c.ts("dve", fneg[:], FC_[:], -1.0, None, ALU.mult, None, [bFC], [b_fneg])
        c.dma("sp", o_cl[:, cols], fneg[:], [b_fneg], (), final=True)
    return c.finish()


RET_LOOKBACK_TILES = [2, 3, 6, 11, 21, 1 << 30]


def build_B(S):
    c = Ctx("B")
    NQ = S // TT
    SK = S // 4
    QPQ = NQ // 4
    fqT = c.din("fqT", [6, 64, S], BF16)
    rqT = c.din("rqT", [6, 64, S], BF16)
    fkT = c.din("fkT", [6, 64, SK], BF16)
    rkT = c.din("rkT", [6, 64, SK], BF16)
    fv = c.din("fv", [6, 128, NQ, 64], BF16)
    rv = c.din("rv", [6, 128, NQ, 64], BF16)
    cl_q = c.din("cl_q", [6, S], F32)
    cl_k = c.din("cl_k", [6, SK], F32)
    tots = c.din("tots", [6, 4], F32)
    dmask_d = c.din("dmask", [128, TT], F32)
    po_fo = c.dout("po_fo", [384, S], F32)
    po_fs = c.dout("po_fs", [6, S], F32)
    po_ro = c.dout("po_ro", [384, S], F32)
    cqp = c.dscratch("cqp", [6, 3, S], BF16)
    ckn = c.dscratch("ckn", [6, 3, SK], BF16)
    b_cqp, b_ckn = Buf(), Buf()

    nm_d = c.din("negmask", [128, TT], F32)
    id_d = c.din("ident", [128, 128], F32)
    negm = c.sb([128, TT], BF16, "negm")
    b_negm = Buf()
    c.dma("pool", negm[:], nm_d, (), [b_negm])
    ident = c.sb([128, 128], BF16, "ident_sb")
    b_ident = Buf()
    c.dma("pool", ident[:], id_d, (), [b_ident])
    dmf = c.sb([128, TT], F32, "dmf")
    b_dmf = Buf()
    c.dma("sp", dmf[:], dmask_d, (), [b_dmf])
    dm = c.sb([128, TT], BF16, "dm")
    b_dm = Buf()
    c.copy("dve", dm[:], dmf[:], [b_dmf], [b_dm])
    tt_ = c.sb([6, 4], F32, "tots_sb")
    b_tt = Buf()
    c.dma("sp", tt_[:], tots, (), [b_tt])
    off = c.sb([6, 4], F32, "off")
    b_off = Buf()
    c.memset("dve", off[:, 0:1], 0.0, [b_off])
    for q in range(1, 4):
        c.tt("dve", off[:, q:q + 1], off[:, q - 1:q], tt_[:, q - 1:q], ALU.add, [b_off, b_tt], [b_off])
    CW = 2048
    cc = c.sb([6, CW], F32, "cc")
    b_cc = Buf()
    cb16 = c.sb([6, 3, CW], BF16, "cb16")
    b_cb16 = Buf()
    cback = c.sb([6, CW], F32, "cback")
    b_cback = Buf()

    def split3(src_d, n, width_per_quarter, dst, b_dst, sign):
        for q in range(4):
            for w0 in range(0, width_per_quarter, CW):
                w = min(CW, width_per_quarter - w0)
                col0 = q * width_per_quarter + w0
                c.dma("sp", cc[:, 0:w], src_d[:, col0:col0 + w], (), [b_cc])
                c.ts("dve", cc[:, 0:w], cc[:, 0:w], off[:, q:q + 1], sign, ALU.add, ALU.mult, [b_cc, b_off], [b_cc])
                for part in range(3):
                    c.copy("dve", cb16[:, part, 0:w], cc[:, 0:w], [b_cc], [b_cb16])
                    if part < 2:
                        c.copy("dve", cback[:, 0:w], cb16[:, part, 0:w], [b_cb16], [b_cback])
                        c.tt("dve", cc[:, 0:w], cc[:, 0:w], cback[:, 0:w], ALU.subtract, [b_cc, b_cback], [b_cc])
                c.dma("sp", dst[:, :, col0:col0 + w], cb16[:, :, 0:w], [b_cb16], [b_dst])

    split3(cl_q, S, S // 4, cqp, b_cqp, 1.0)
    split3(cl_k, SK, SK // 4, ckn, b_ckn, -1.0)

    KE = [c.sb([70, SK], BF16, "KE%d" % i) for i in range(2)]
    b_KE = [Buf() for _ in range(2)]
    VE = [c.sb([128, NQ, 65], BF16, "VE%d" % i) for i in range(2)]
    b_VE = [Buf() for _ in range(2)]
    for i in range(2):
        c.memset("pool", KE[i][64:70, :], 1.0, [b_KE[i]])
        c.memset("pool", VE[i][:, :, 64:65], 1.0, [b_VE[i]])
    NQB = 3
    QE = [c.sb([70, TT], BF16, "QE%d" % i) for i in range(NQB)]
    b_QE = [Buf() for _ in range(NQB)]
    for i in range(NQB):
        c.memset("pool", QE[i][64:70, :], 1.0, [b_QE[i]])
    NPB = 4
    PB = [c.sb([128, TT], BF16, "PB%d" % i) for i in range(NPB)]
    b_PB = [Buf() for _ in range(NPB)]
    NS = 3
    sp_ = [c.ps(name="sps%d" % i) for i in range(NS)]
    b_sp = [Buf() for _ in range(NS)]
    op_ = [c.ps(name="ops%d" % i) for i in range(2)]
    b_op = [Buf() for _ in range(2)]
    osb = [c.sb([65, TT], F32, "osb%d" % i) for i in range(2)]
    b_osb = [Buf() for _ in range(2)]
    heads = [(typ, h) for typ in ("fox", "ret") for h in range(6)]
    jobs = []
    for hi, (typ, h) in enumerate(heads):
        for i in range(NQ):
            lo = 0 if typ == "fox" else max(0, i - RET_LOOKBACK_TILES[h])
            jobs.append({"hi": hi, "typ": typ, "h": h, "i": i, "lo": lo})
    steps = []
    for jn, jb in enumerate(jobs):
        for m in range(jb["lo"], jb["i"] + 1):
            steps.append((jn, m))

    def load_head(hi):
        typ, h = heads[hi]
        K_, bK, V_, bV = KE[hi % 2], b_KE[hi % 2], VE[hi % 2], b_VE[hi % 2]
        if typ == "fox":
            c.dma("sp", K_[0:64, :], fkT[h], (), [bK])
            c.dma("sp", K_[64:67, :], ckn[h], [b_ckn], [bK])
            c.dma("sp", V_[:, :, 0:64], fv[h], (), [bV])
        else:
            c.dma("sp", K_[0:64, :], rkT[h], (), [bK])
            c.dma("sp", V_[:, :, 0:64], rv[h], (), [bV])

    def load_job(jn):
        jb = jobs[jn]
        Q_, bQ = QE[jn % NQB], b_QE[jn % NQB]
        cols = slice(jb["i"] * TT, (jb["i"] + 1) * TT)
        if jb["typ"] == "fox":
            c.dma("sp", Q_[0:64, :], fqT[jb["h"]][:, cols], (), [bQ])
            c.dma("sp", Q_[67:70, :], cqp[jb["h"]][:, cols], [b_cqp], [bQ])
        else:
            c.dma("sp", Q_[0:64, :], rqT[jb["h"]][:, cols], (), [bQ])

    def s_part(n):
        jn, m = steps[n]
        jb = jobs[jn]
        typ, h, i, hi = jb["typ"], jb["h"], jb["i"], jb["hi"]
        if m == jb["lo"]:
            if jn == 0:
                load_head(0)
                load_job(0)
                if len(jobs) > 1:
                    load_job(1)
            if jn + 2 < len(jobs):
                load_job(jn + 2)
        K_, bK = KE[hi % 2], b_KE[hi % 2]
        Q_, bQ = QE[jn % NQB], b_QE[jn % NQB]
        S_, bS = sp_[n % NS], b_sp[n % NS]
        P_, bP = PB[n % NPB], b_PB[n % NPB]
        rows = 70 if typ == "fox" else 64
        diag = (m == i)
        if typ == "fox":
            c.mm(S_[:], K_[0:rows, m * 128:(m + 1) * 128], Q_[0:rows, :], True, not diag, [bK, bQ], [bS])
            if diag:
                c.mm(S_[:], ident[:], negm[:], False, True, [b_ident, b_negm], [bS])
            c.act(P_[:], S_[:], AF.Exp, [bS], [bP])
        else:
            c.mm(S_[:], K_[0:rows, m * 128:(m + 1) * 128], Q_[0:rows, :], True, True, [bK, bQ], [bS])
            cst = math.exp(LOG_G[h] * (512.0 * (i - m) - 127.0))
            c.ts("dve", P_[:], S_[:], cst, None, ALU.mult, None, [bS], [bP])
            if diag:
                c.tt("pool", P_[:], P_[:], dm[:], ALU.mult, [bP, b_dm], [bP])

    def pv_part(n):
        jn, m = steps[n]
        jb = jobs[jn]
        typ, h, i, hi = jb["typ"], jb["h"], jb["i"], jb["hi"]
        V_, bV = VE[hi % 2], b_VE[hi % 2]
        P_, bP = PB[n % NPB], b_PB[n % NPB]
        O_, bO = op_[jn % 2], b_op[jn % 2]
        vcols = 65 if typ == "fox" else 64
        if i == 0 and m == jb["lo"] and hi + 1 < len(heads):
            load_head(hi + 1)
        c.mm(O_[0:vcols, :], V_[:, m, 0:vcols], P_[:], m == jb["lo"], m == i, [bV, bP], [bO])
        if m == i:
            cols = slice(i * TT, (i + 1) * TT)
            OS, bOS = osb[jn % 2], b_osb[jn % 2]
            c.copy("act" if typ == "ret" else "dve", OS[0:vcols, :], O_[0:vcols, :], [bO], [bOS])
            if typ == "fox":
                c.dma("sp", po_fo[h * 64:(h + 1) * 64, cols], OS[0:64, :], [bOS], (), final=True)
                c.dma("sp", po_fs[h:h + 1, cols], OS[64:65, :], [bOS], (), final=True)
            else:
                c.dma("sp", po_ro[h * 64:(h + 1) * 64, cols], OS[0:64, :], [bOS], (), final=True)

    LOOK = 2
    for n in range(len(steps) + LOOK):
        if n < len(steps):
            s_part(n)
        if n - LOOK >= 0:
            pv_part(n - LOOK)
    return c.finish()


LOG_G = [math.log(1.0 - 2.0 ** (-5.0 - h)) for h in range(6)]


def make_consts():
    p = np.arange(128)
    invf = (10000.0 ** (-((p % 64) % 32) / 32.0)).astype(np.float32).reshape(128, 1)
    t = np.arange(TT)
    gq = np.zeros((3, 128, TT), np.float64)
    gk = np.zeros((3, 128, TT), np.float64)
    for j in range(3):
        for half in range(2):
            lg = LOG_G[2 * j + half]
            gq[j, half * 64:(half + 1) * 64, :] = np.exp(lg * t)[None, :]
            gk[j, half * 64:(half + 1) * 64, :] = 0.125 * np.exp(lg * (127 - t))[None, :]
    tri = (p[None, :] >= p[:, None]).astype(np.float32)
    e2 = np.zeros((2, 128), np.float32)
    e2[0, :64] = 1.0
    e2[1, 64:] = 1.0
    return {"invf": invf, "gq": gq.astype(np.float32), "gk": gk.astype(np.float32), "tri": tri, "e2": e2}


_PROGS = {}


def _prog(name, *args):
    key = (name,) + args
    if key not in _PROGS:
        _PROGS[key] = {"A": build_A, "B": build_B, "C": build_C}[name](*args)
    return _PROGS[key]


def _fm(v):
    return np.ascontiguousarray(np.asarray(v, np.float32).reshape(-1, 128).T)


def _lay_w(w, nin_c, nout_c):
    return np.ascontiguousarray(np.asarray(w, np.float32).reshape(nin_c, 128, nout_c, 128).transpose(2, 1, 0, 3))


def _run(nc, in_maps):
    res = run_bass_kernel_spmd(nc, in_maps, core_ids=list(range(len(in_maps))))
    return res.results


def kernel(x, p, positions, mix_pre_g, w_in, sgu_v_g, sgu_w, sgu_b, fox_b_f, w_o, mix_post_g, ffn_pre_g,
           w_gate, w_up, conv_w, conv_b, w_down, ffn_post_g, ple_pre_g, w_ple_gate, w_ple_proj, ple_post_g):
    x = np.asarray(x, np.float32)
    p = np.asarray(p, np.float32)
    positions = np.asarray(positions, np.int32)
    B, S, _ = x.shape
    depth = p.shape[0]
    NT = S // 4
    NQ = S // TT
    ncore = B * 4
    consts = make_consts()
    pp = np.arange(128)
    selb = np.zeros((6, 3, 128), np.float32)
    for j in range(3):
        for q in range(128):
            selb[2 * j + q // 64, j, q] = 1.0
    blk = (pp[:, None] // 64 == pp[None, :] // 64).astype(np.float32)
    tq = np.arange(TT)
    dmask = [((128 * j + pp)[:, None] <= tq[None, :]).astype(np.float32) for j in range(4)]
    x_cur = x.copy()
    for i in range(depth):
        a_common = {
            "w_in": np.ascontiguousarray(np.asarray(w_in[i], np.float32)),
            "mix_pre_g": _fm(mix_pre_g[i]),
            "sgu_vg_b": np.ascontiguousarray(np.broadcast_to(np.asarray(sgu_v_g[i], np.float32).reshape(1, 256), (128, 256))),
            "sgu_wT": np.ascontiguousarray(np.asarray(sgu_w[i], np.float32).transpose(0, 2, 1)),
            "sgu_b": np.ascontiguousarray(np.asarray(sgu_b[i], np.float32).reshape(2, 2, 128)),
            "fox_bf": np.ascontiguousarray(np.asarray(fox_b_f[i], np.float32).reshape(6, 1)),
            "invf": consts["invf"], "gq": consts["gq"], "gk": consts["gk"], "tri": consts["tri"], "e2": consts["e2"],
        }
        xTs = []
        maps = []
        for b in range(B):
            for j in range(4):
                sl = slice(j * NT, (j + 1) * NT)
                xT = np.ascontiguousarray(x_cur[b, sl].T)
                xTs.append(xT)
                maps.append(dict(a_common, xT=xT,
                                 posb=np.ascontiguousarray(np.broadcast_to(positions[b, sl][None, :], (128, NT)))))
        rA = _run(_prog("A", NT, 1), maps)
        maps = []
        for b in range(B):
            cat = lambda k, ax: np.concatenate([np.asarray(rA[b * 4 + j][k]) for j in range(4)], axis=ax)
            fq, rq, fk, rk = cat("fqT", 1), cat("rqT", 1), cat("fkT", 1), cat("rkT", 1)
            vt = cat("vtok", 0)
            cl = cat("cl", 1)
            tots = np.ascontiguousarray(np.stack([np.asarray(rA[b * 4 + j]["cl"])[:, -1] for j in range(4)], axis=1))
            for j in range(4):
                selk = lambda a: np.ascontiguousarray(a.reshape(6, 64, NQ, 4, 128)[:, :, :, j, :].reshape(6, 64, S // 4))
                selv = lambda a: np.ascontiguousarray(a.reshape(NQ, 4, 128, 6, 64)[:, j].transpose(2, 1, 0, 3))
                maps.append({
                    "fqT": np.ascontiguousarray(fq.reshape(6, 64, S)), "rqT": np.ascontiguousarray(rq.reshape(6, 64, S)),
                    "fkT": selk(fk), "rkT": selk(rk),
                    "fv": selv(vt[:, 384:768]), "rv": selv(vt[:, 0:384]),
                    "cl_q": np.ascontiguousarray(cl),
                    "cl_k": np.ascontiguousarray(cl.reshape(6, NQ, 4, 128)[:, :, j, :].reshape(6, S // 4)),
                    "tots": tots, "dmask": dmask[j], "negmask": (dmask[j] - 1.0) * 30000.0,
                    "ident": np.eye(128, dtype=np.float32),
                })
        rB = _run(_prog("B", S), maps)
        c_common = {
            "w_o": _lay_w(w_o[i], 8, 8), "mix_post_g": _fm(mix_post_g[i]), "selb": selb, "blk": blk,
            "w_gate": _lay_w(w_gate[i], 8, 32), "w_up": _lay_w(w_up[i], 8, 32), "w_down": _lay_w(w_down[i], 32, 8),
            "w_ple_gate": _lay_w(w_ple_gate[i], 8, 8), "w_ple_proj": _lay_w(w_ple_proj[i], 2, 8),
            "conv_w": np.ascontiguousarray(np.asarray(conv_w[i], np.float32).T.reshape(32, 128, 3).transpose(1, 0, 2)),
            "conv_b": np.ascontiguousarray(np.asarray(conv_b[i], np.float32).reshape(32, 128).T),
            "ffn_pre_g": _fm(ffn_pre_g[i]), "ffn_post_g": _fm(ffn_post_g[i]),
            "ple_pre_g": _fm(ple_pre_g[i]), "ple_post_g": _fm(ple_post_g[i]),
        }
        maps = []
        for b in range(B):
            pf = [np.asarray(rB[b * 4 + r]["po_fo"]) for r in range(4)]
            ps_ = [np.asarray(rB[b * 4 + r]["po_fs"]) for r in range(4)]
            pr = [np.asarray(rB[b * 4 + r]["po_ro"]) for r in range(4)]
            for j in range(4):
                sl = slice(j * NT, (j + 1) * NT)
                m = dict(c_common)
                m["xT"] = xTs[b * 4 + j]
                m["po_fo"] = np.ascontiguousarray(np.stack([a[:, sl] for a in pf]))
                m["po_fs"] = np.ascontiguousarray(np.stack([a[:, sl] for a in ps_]))
                m["po_ro"] = np.ascontiguousarray(np.stack([a[:, sl] for a in pr]))
                m["out_aT"] = np.asarray(rA[b * 4 + j]["out_aT"])
                m["rgT"] = np.asarray(rA[b * 4 + j]["rgT"])
                m["pT"] = np.ascontiguousarray(p[i, b, sl].T)
                if j > 0:
                    hs = slice(j * NT - 2, j * NT)
                    m["h_x"] = np.ascontiguousarray(x_cur[b, hs].T)
                    m["h_fo"] = np.ascontiguousarray(np.stack([a[:, hs] for a in pf]))
                    m["h_fs"] = np.ascontiguousarray(np.stack([a[:, hs] for a in ps_]))
                    m["h_ro"] = np.ascontiguousarray(np.stack([a[:, hs] for a in pr]))
                    m["h_oa"] = np.ascontiguousarray(np.asarray(rA[b * 4 + j - 1]["out_aT"])[:, -2:])
                    m["h_rg"] = np.ascontiguousarray(np.asarray(rA[b * 4 + j - 1]["rgT"])[:, -2:])
                else:
                    m["h_x"] = np.zeros((D, 2), np.float32)
                    m["h_fo"] = np.zeros((4, 384, 2), np.float32)
                    m["h_fs"] = np.ones((4, 6, 2), np.float32)
                    m["h_ro"] = np.zeros((4, 384, 2), np.float32)
                    m["h_oa"] = np.zeros((256, 2), ml_dtypes.bfloat16)
                    m["h_rg"] = np.zeros((384, 2), ml_dtypes.bfloat16)
                maps.append(m)
        rC = _run(_prog("C", NT), maps)
        for b in range(B):
            for j in range(4):
                x_cur[b, j * NT:(j + 1) * NT] = np.asarray(rC[b * 4 + j]["outT"]).T
    return x_cur
```
